# Optimizing a Trainium2 kernel written in Bass

```python
import math
import jax, jax.numpy as jnp
from jax import lax
import numpy as np

D_MODEL = 2048
BATCH = 16
SEQ = 2048
DEPTH = 2

A_HEADS = 8
A_HEAD_DIM = 64
A_Q_RANK = 384
A_KV_RANK = 256
IDX_HEADS = 16
IDX_DIM = 64
DSA_TOPK = 256
DSA_Q_BLOCK = 128
SSM_GROUP = 16
SSM_GROUPS = 32
SSM_WIDTH = SSM_GROUP * SSM_GROUPS
SSM_STATE = 64
C_HEADS = 8
C_HEAD_DIM = 64
MOBA_BLOCK = 256
MOBA_TOPK = 3
MOBA_Q_BLOCK = 32
N_BRANCH = 3
BRANCH_WIDTH = 512
D_FF = 256 * math.ceil(8 * D_MODEL / 3 / 256)
CONV_WIDTH = 3
REL_BUCKETS = 32
REL_MAX_DIST = 128
LN_EPS = 1e-5
NEG_INF = -1e30
DEEPNORM_ALPHA = (2 * DEPTH) ** 0.25
DEEPNORM_BETA = (8 * DEPTH) ** -0.25
IN_SPLITS = (A_Q_RANK, A_KV_RANK, IDX_DIM, IDX_HEADS, SSM_WIDTH,
             C_HEADS * C_HEAD_DIM, C_HEADS * C_HEAD_DIM, C_HEADS * C_HEAD_DIM,
             N_BRANCH * D_MODEL)
IN_WIDTH = sum(IN_SPLITS)

kernel_name = 'dsa_s5_moba_gated_hybrid'

F32 = jnp.float32


def _split_points(sizes):
    pts, acc = [], 0
    for s in sizes[:-1]:
        acc += s
        pts.append(acc)
    return pts


def layer_norm(x, g, b):
    xf = x.astype(F32)
    mu = xf.mean(-1, keepdims=True)
    var = jnp.square(xf - mu).mean(-1, keepdims=True)
    return ((xf - mu) * lax.rsqrt(var + LN_EPS) * g.astype(F32) + b.astype(F32)).astype(x.dtype)


def rms_norm(x, g):
    xf = x.astype(F32)
    return (xf * lax.rsqrt(jnp.mean(xf * xf, -1, keepdims=True) + LN_EPS) * g.astype(F32)).astype(x.dtype)


def t5_bucket(dist):
    n = jnp.maximum(dist, 0)
    exact = REL_BUCKETS // 2
    log_ratio = jnp.log(jnp.maximum(n, 1).astype(F32) / exact) / math.log(REL_MAX_DIST / exact)
    large = jnp.minimum(exact + (log_ratio * (REL_BUCKETS - exact)).astype(jnp.int32), REL_BUCKETS - 1)
    return jnp.where(n < exact, n, large)


def to_chunks(a, q):
    return jnp.moveaxis(a.reshape(a.shape[0], a.shape[1] // q, q, *a.shape[2:]), 1, 0)


def from_chunks(a):
    a = jnp.moveaxis(a, 0, 1)
    return a.reshape(a.shape[0], a.shape[1] * a.shape[2], *a.shape[3:])


def dsa_mixer(c_q, c_kv, k_idx, w_idx, w_uq, w_uk, w_uv, w_qidx, bias_tab):
    Bn, T, _ = c_q.shape
    n_top = min(DSA_TOPK, T // 4)
    q = jnp.einsum('btr,rhd->bthd', c_q, w_uq)
    q_lat = jnp.einsum('bthd,chd->bthc', q, w_uk) * A_HEAD_DIM ** -0.5
    q_idx = jnp.einsum('btr,rhd->bthd', c_q, w_qidx) * IDX_DIM ** -0.5
    w_idx = w_idx * IDX_HEADS ** -0.5
    key_pos = jnp.arange(T)

    def attend(args):
        ql, qi, wi, qpos = args
        rel = jax.nn.relu(jnp.einsum('bqhd,bsd->bqhs', qi, k_idx))
        score = jnp.einsum('bqh,bqhs->bqs', wi, rel).astype(F32)
        score = jnp.where(qpos[:, None] >= key_pos[None, :], score, -jnp.inf)
        _, idx = lax.top_k(score, n_top)
        kv = jax.vmap(lambda c, i: c[i])(c_kv, idx)
        dist = qpos[None, :, None] - idx
        logits = (jnp.einsum('bqhc,bqkc->bqhk', ql, kv).astype(F32)
                  + jnp.moveaxis(bias_tab[t5_bucket(dist)].astype(F32), -1, 2))
        logits = jnp.where((dist >= 0)[:, :, None, :], logits, NEG_INF)
        p = jax.nn.softmax(logits, axis=-1).astype(c_kv.dtype)
        return jnp.einsum('bqhk,bqkc->bqhc', p, kv)

    pos = jnp.arange(T).reshape(T // DSA_Q_BLOCK, DSA_Q_BLOCK)
    o_lat = from_chunks(lax.map(attend, (to_chunks(q_lat, DSA_Q_BLOCK), to_chunks(q_idx, DSA_Q_BLOCK),
                                         to_chunks(w_idx, DSA_Q_BLOCK), pos)))
    o = jnp.einsum('bthc,chd->bthd', o_lat, w_uv)
    return o.reshape(Bn, T, A_HEADS * A_HEAD_DIM)


def s5_mixer(u, lam_re, lam_im, log_step, b_re, b_im, c_re, c_im, d_skip, w_glu, b_glu):
    Bn, T, _ = u.shape
    uf = u.astype(F32).reshape(Bn, T, SSM_GROUPS, SSM_GROUP)
    lam = lax.complex(lam_re.astype(F32), lam_im.astype(F32))
    step = jnp.exp(log_step.astype(F32))[:, None]
    lam_bar = jnp.exp(lam * step)
    b_bar = ((lam_bar - 1.0) / lam)[:, :, None] * lax.complex(b_re.astype(F32), b_im.astype(F32))
    bu = jnp.einsum('btgp,gnp->btgn', uf.astype(jnp.complex64), b_bar)
    a = jnp.broadcast_to(lam_bar, (1, T) + lam_bar.shape)

    def combine(l, r):
        return l[0] * r[0], r[0] * l[1] + r[1]

    _, state = lax.associative_scan(combine, (a, bu), axis=1)
    c = lax.complex(c_re.astype(F32), c_im.astype(F32))
    y = jnp.real(jnp.einsum('btgn,gpn->btgp', state, c)) + d_skip.astype(F32).reshape(SSM_GROUPS, SSM_GROUP) * uf
    y = jax.nn.gelu(y.reshape(Bn, T, SSM_WIDTH)).astype(u.dtype)
    return y * jax.nn.sigmoid(y @ w_glu + b_glu)


def moba_mixer(q, k, v, bias_tab):
    Bn, T, H, Dh = q.shape
    n_blk = -(-T // MOBA_BLOCK)
    pad = ((0, 0), (0, n_blk * MOBA_BLOCK - T), (0, 0), (0, 0))
    k_p, v_p = jnp.pad(k, pad), jnp.pad(v, pad)
    k_mean = k_p.astype(F32).reshape(Bn, n_blk, MOBA_BLOCK, H, Dh).mean(2)
    own = jnp.arange(T) // MOBA_BLOCK
    gate = jnp.einsum('bthd,bnhd->bthn', q.astype(F32), k_mean)
    past = jnp.arange(n_blk)[None, :] < own[:, None]
    gate = jnp.where(past[None, :, None, :], gate, -jnp.inf)
    n_sel = min(MOBA_TOPK, n_blk)
    _, sel = lax.top_k(gate, n_sel)
    k_bh = jnp.transpose(k_p.reshape(Bn, n_blk, MOBA_BLOCK, H, Dh), (0, 3, 1, 2, 4))
    v_bh = jnp.transpose(v_p.reshape(Bn, n_blk, MOBA_BLOCK, H, Dh), (0, 3, 1, 2, 4))
    scale = Dh ** -0.5
    tab_t = bias_tab.astype(F32).T
    head_ix = jnp.arange(H)[None, :, None, None, None]
    gather = jax.vmap(jax.vmap(lambda blocks, s: blocks[s]))

    def attend(args):
        qc, selc, start = args
        qpos = start + jnp.arange(MOBA_Q_BLOCK)
        own_blk = start // MOBA_BLOCK
        ob = own_blk * MOBA_BLOCK
        k_own = lax.dynamic_slice_in_dim(k_p, ob, MOBA_BLOCK, axis=1)
        v_own = lax.dynamic_slice_in_dim(v_p, ob, MOBA_BLOCK, axis=1)
        d_own = qpos[:, None] - (ob + jnp.arange(MOBA_BLOCK))[None, :]
        l_own = jnp.einsum('bqhd,bshd->bhqs', qc, k_own).astype(F32) * scale + tab_t[:, t5_bucket(d_own)]
        l_own = jnp.where(d_own >= 0, l_own, NEG_INF)
        sel_bh = jnp.transpose(selc, (0, 2, 1, 3))
        k_sel = gather(k_bh, sel_bh)
        v_sel = gather(v_bh, sel_bh)
        d_sel = qpos[:, None, None] - (sel_bh[..., None] * MOBA_BLOCK + jnp.arange(MOBA_BLOCK))
        l_sel = (jnp.einsum('bqhd,bhqjsd->bhqjs', qc, k_sel).astype(F32) * scale
                 + tab_t[head_ix, t5_bucket(d_sel)])
        l_sel = jnp.where((sel_bh < own_blk)[..., None], l_sel, NEG_INF)
        logits = jnp.concatenate([l_own, l_sel.reshape(Bn, H, MOBA_Q_BLOCK, n_sel * MOBA_BLOCK)], axis=-1)
        p = jax.nn.softmax(logits, axis=-1).astype(v.dtype)
        p_own = p[..., :MOBA_BLOCK]
        p_sel = p[..., MOBA_BLOCK:].reshape(Bn, H, MOBA_Q_BLOCK, n_sel, MOBA_BLOCK)
        return (jnp.einsum('bhqs,bshd->bqhd', p_own, v_own)
                + jnp.einsum('bhqjs,bhqjsd->bqhd', p_sel, v_sel))

    starts = jnp.arange(T // MOBA_Q_BLOCK) * MOBA_Q_BLOCK
    o = from_chunks(lax.map(attend, (to_chunks(q, MOBA_Q_BLOCK), to_chunks(sel, MOBA_Q_BLOCK), starts)))
    return o.reshape(Bn, T, H * Dh)


def mixer_block(x, rel_bias, w_in, cq_gain, ckv_gain, w_uq, w_uk, w_uv, w_qidx,
                lam_re, lam_im, log_step, b_re, b_im, c_re, c_im, d_skip, w_glu, b_glu,
                w_branch, w_out):
    Bn, T, _ = x.shape
    h = jnp.einsum('btd,dn->btn', x, w_in)
    c_q, c_kv, k_idx, w_idx, u, q_c, k_c, v_c, g = jnp.split(h, _split_points(IN_SPLITS), axis=-1)
    o_a = dsa_mixer(rms_norm(c_q, cq_gain), rms_norm(c_kv, ckv_gain), k_idx, w_idx,
                    w_uq, w_uk, w_uv, w_qidx, rel_bias[:, :A_HEADS])
    o_b = s5_mixer(u, lam_re, lam_im, log_step, b_re, b_im, c_re, c_im, d_skip, w_glu, b_glu)
    head_shape = (Bn, T, C_HEADS, C_HEAD_DIM)
    o_c = moba_mixer(q_c.reshape(head_shape), k_c.reshape(head_shape), v_c.reshape(head_shape),
                     rel_bias[:, A_HEADS:])
    o = jnp.stack([o_a, o_b, o_c], axis=2)
    y = jnp.einsum('btnc,ncd->btnd', o, w_branch)
    gate = jax.nn.sigmoid(g.reshape(Bn, T, N_BRANCH, D_MODEL))
    merged = jnp.einsum('btnd,btnd->btd', gate, y)
    return merged @ w_out


def conv_ffn(x, w_up, conv_w, conv_b, w_down):
    T = x.shape[1]
    h = x @ w_up
    hp = jnp.pad(h, ((0, 0), (CONV_WIDTH - 1, 0), (0, 0)))
    h = conv_b + sum(hp[:, j:j + T] * conv_w[j] for j in range(CONV_WIDTH))
    a, val = jnp.split(h, 2, axis=-1)
    return (jax.nn.gelu(a) * val) @ w_down


def setup_inputs(seed: int = 0) -> dict:
    key = jax.random.key(seed)
    ks = iter(jax.random.split(key, 40))

    def nrm(shape, scale):
        return jax.random.normal(next(ks), shape, F32) * scale

    L = DEPTH
    beta = DEEPNORM_BETA
    x = nrm((BATCH, SEQ, D_MODEL), 1.0)
    rel_bias = nrm((REL_BUCKETS, A_HEADS + C_HEADS), 0.2)
    w_in = nrm((L, D_MODEL, IN_WIDTH), D_MODEL ** -0.5)
    cq_gain = 1.0 + nrm((L, A_Q_RANK), 0.01)
    ckv_gain = 1.0 + nrm((L, A_KV_RANK), 0.01)
    w_uq = nrm((L, A_Q_RANK, A_HEADS, A_HEAD_DIM), A_Q_RANK ** -0.5)
    w_uk = nrm((L, A_KV_RANK, A_HEADS, A_HEAD_DIM), A_KV_RANK ** -0.5)
    w_uv = nrm((L, A_KV_RANK, A_HEADS, A_HEAD_DIM), beta * A_KV_RANK ** -0.5)
    w_qidx = nrm((L, A_Q_RANK, IDX_HEADS, IDX_DIM), A_Q_RANK ** -0.5)
    lam_re = -0.5 + nrm((L, SSM_GROUPS, SSM_STATE), 0.01)
    lam_im = math.pi * jnp.arange(SSM_STATE, dtype=F32) + nrm((L, SSM_GROUPS, SSM_STATE), 0.01)
    log_step = jax.random.uniform(next(ks), (L, SSM_GROUPS), F32, math.log(1e-3), math.log(1e-1))
    b_re = nrm((L, SSM_GROUPS, SSM_STATE, SSM_GROUP), (2 * SSM_GROUP) ** -0.5)
    b_im = nrm((L, SSM_GROUPS, SSM_STATE, SSM_GROUP), (2 * SSM_GROUP) ** -0.5)
    c_re = nrm((L, SSM_GROUPS, SSM_GROUP, SSM_STATE), 0.5)
    c_im = nrm((L, SSM_GROUPS, SSM_GROUP, SSM_STATE), 0.5)
    d_skip = nrm((L, SSM_WIDTH), 1.0)
    w_glu = nrm((L, SSM_WIDTH, SSM_WIDTH), SSM_WIDTH ** -0.5)
    b_glu = nrm((L, SSM_WIDTH), 0.01)
    w_branch = nrm((L, N_BRANCH, BRANCH_WIDTH, D_MODEL), beta * BRANCH_WIDTH ** -0.5)
    w_out = nrm((L, D_MODEL, D_MODEL), beta * D_MODEL ** -0.5)
    ln1_g = 1.0 + nrm((L, D_MODEL), 0.01)
    ln1_b = nrm((L, D_MODEL), 0.01)
    w_up = nrm((L, D_MODEL, 2 * D_FF), beta * D_MODEL ** -0.5)
    conv_w = nrm((L, CONV_WIDTH, 2 * D_FF), CONV_WIDTH ** -0.5)
    conv_b = nrm((L, 2 * D_FF), 0.01)
    w_down = nrm((L, D_FF, D_MODEL), beta * D_FF ** -0.5)
    ln2_g = 1.0 + nrm((L, D_MODEL), 0.01)
    ln2_b = nrm((L, D_MODEL), 0.01)
    return {'x': x, 'rel_bias': rel_bias, 'w_in': w_in, 'cq_gain': cq_gain, 'ckv_gain': ckv_gain,
            'w_uq': w_uq, 'w_uk': w_uk, 'w_uv': w_uv, 'w_qidx': w_qidx,
            'lam_re': lam_re, 'lam_im': lam_im, 'log_step': log_step, 'b_re': b_re, 'b_im': b_im,
            'c_re': c_re, 'c_im': c_im, 'd_skip': d_skip, 'w_glu': w_glu, 'b_glu': b_glu,
            'w_branch': w_branch, 'w_out': w_out, 'ln1_g': ln1_g, 'ln1_b': ln1_b,
            'w_up': w_up, 'conv_w': conv_w, 'conv_b': conv_b, 'w_down': w_down,
            'ln2_g': ln2_g, 'ln2_b': ln2_b}


def reference(x, rel_bias, w_in, cq_gain, ckv_gain, w_uq, w_uk, w_uv, w_qidx,
              lam_re, lam_im, log_step, b_re, b_im, c_re, c_im, d_skip, w_glu, b_glu,
              w_branch, w_out, ln1_g, ln1_b, w_up, conv_w, conv_b, w_down, ln2_g, ln2_b):
    for l in range(DEPTH):
        mix = mixer_block(x, rel_bias, w_in[l], cq_gain[l], ckv_gain[l], w_uq[l], w_uk[l], w_uv[l], w_qidx[l],
                          lam_re[l], lam_im[l], log_step[l], b_re[l], b_im[l], c_re[l], c_im[l],
                          d_skip[l], w_glu[l], b_glu[l], w_branch[l], w_out[l])
        x = layer_norm(DEEPNORM_ALPHA * x + mix, ln1_g[l], ln1_b[l])
        x = layer_norm(DEEPNORM_ALPHA * x + conv_ffn(x, w_up[l], conv_w[l], conv_b[l], w_down[l]),
                       ln2_g[l], ln2_b[l])
    return x
```

```python
import math
from contextlib import ExitStack
import numpy as np
import concourse.bass as bass
import concourse.mybir as mybir
from concourse.bass_utils import run_bass_kernel_spmd

F32 = mybir.dt.float32
BF16 = mybir.dt.bfloat16
I32 = mybir.dt.int32
ALU = mybir.AluOpType
AF = mybir.ActivationFunctionType
AX = mybir.AxisListType

NCORES = 8
NB = 2
T = 2048
D = 2048
DEPTH = 2
INW = 8912
DFF = 5632
ALPHA = (2 * DEPTH) ** 0.25
EPS = 1e-5
C_CQ, C_CKV, C_KIDX, C_WIDX, C_U, C_QC, C_KC, C_VC, C_G = 0, 384, 640, 704, 720, 1232, 1744, 2256, 2768
NCH = 20


class Tl:
    __slots__ = ("name", "ap", "writers", "readers")

    def __init__(self, name, ap=None):
        self.name = name
        self.ap = ap
        self.writers = []
        self.readers = []

    def __getitem__(self, idx):
        return self.ap[idx]


class Op:
    __slots__ = ("eng", "fn", "deps", "signal", "sem", "val", "is_dma", "chan")

    def __init__(self, eng, fn, is_dma=False):
        self.eng = eng
        self.fn = fn
        self.deps = []
        self.signal = False
        self.sem = None
        self.val = None
        self.is_dma = is_dma
        self.chan = None


class Sched:
    def __init__(self, nc):
        self.nc = nc
        self.ops = []
        self.chan_last = [None] * NCH
        self.chan_rr = 0

    def _add(self, op, reads, writes, extra=()):
        deps = {}

        def add_dep(p):
            if p is op:
                return
            if (not p.is_dma) and (not op.is_dma) and p.eng == "pe" and op.eng == "pe":
                return
            deps[id(p)] = p

        for p in extra:
            add_dep(p)
        for t in reads:
            for w in t.writers:
                add_dep(w)
        for t in writes:
            for w in t.writers:
                add_dep(w)
            for r in t.readers:
                add_dep(r)
        for t in reads:
            t.readers.append(op)
        for t in writes:
            if t.readers:
                t.readers = []
                t.writers = [op]
            else:
                if not op.is_dma:
                    t.writers = [w for w in t.writers if w.is_dma or w.eng != op.eng]
                t.writers.append(op)
        op.deps = list(deps.values())
        for p in op.deps:
            p.signal = True
        self.ops.append(op)
        return op

    def op(self, eng, fn, reads=(), writes=()):
        return self._add(Op(eng, fn), reads, writes)

    def dma(self, q, out_ap, in_ap, reads=(), writes=(), **kw):
        def fn(e):
            return e.dma_start(out=out_ap, in_=in_ap, **kw)
        o = Op(q, fn, is_dma=True)
        o.signal = True
        ch = self.chan_rr
        self.chan_rr = (self.chan_rr + 1) % NCH
        o.chan = ch
        prev = self.chan_last[ch]
        self.chan_last[ch] = o
        return self._add(o, reads, writes, extra=[prev] if prev is not None else [])

    def dma_like(self, q, fn, reads=(), writes=()):
        o = Op(q, fn, is_dma=True)
        o.signal = True
        ch = self.chan_rr
        self.chan_rr = (self.chan_rr + 1) % NCH
        o.chan = ch
        prev = self.chan_last[ch]
        self.chan_last[ch] = o
        return self._add(o, reads, writes, extra=[prev] if prev is not None else [])

    def emit(self, final_ops=()):
        nc = self.nc
        with ExitStack() as es:
            engs = ("pe", "act", "dve", "pool", "sp")
            esem = {e: es.enter_context(nc.semaphore("s_" + e)) for e in engs}
            csem = [es.enter_context(nc.semaphore("c%d" % i)) for i in range(NCH)]
            ecnt = {e: 0 for e in engs}
            ccnt = [0] * NCH
            for o in self.ops:
                if o.is_dma:
                    ccnt[o.chan] += 16
                    o.sem = csem[o.chan]
                    o.val = ccnt[o.chan]
                elif o.signal:
                    ecnt[o.eng] += 1
                    o.sem = esem[o.eng]
                    o.val = ecnt[o.eng]
            block = es.enter_context(nc.Block())
            engmap = {"pe": block.tensor, "act": block.scalar, "dve": block.vector,
                      "pool": block.gpsimd, "sp": block.sync}
            for ename in engs:
                my = [o for o in self.ops if o.eng == ename]

                def body(e, my=my, ename=ename):
                    waited = {}
                    for o in my:
                        need = {}
                        for p in o.deps:
                            sid = id(p.sem)
                            if waited.get(sid, 0) >= p.val:
                                continue
                            if sid not in need or need[sid][1] < p.val:
                                need[sid] = (p.sem, p.val)
                        for sid, (sm, v) in need.items():
                            e.wait_ge(sm, v)
                            waited[sid] = v
                        ins = o.fn(e)
                        if o.sem is not None:
                            ins.then_inc(o.sem, 16 if o.is_dma else 1)
                    if ename == "sp":
                        for p in final_ops:
                            if waited.get(id(p.sem), 0) < p.val:
                                e.wait_ge(p.sem, p.val)
                                waited[id(p.sem)] = p.val
                engmap[ename](body)


class Arena:
    def __init__(self, ap, ncols):
        self.ap = ap
        self.ncols = ncols
        self.top = 0
        self.dead = []
        self.live = []

    def alloc(self, name, nelem, dtype=F32, parts=128):
        size = 4 if dtype in (F32, I32) else 2
        n4 = (nelem * size + 3) // 4
        n4 = (n4 + 1) // 2 * 2
        off = self.top
        assert off + n4 <= self.ncols, "SBUF arena overflow at %s: %d+%d>%d" % (name, off, n4, self.ncols)
        self.top = off + n4
        ap = self.ap[:, off:off + n4]
        if dtype != F32:
            ap = ap.bitcast(dtype)
        ap = ap[0:parts, 0:nelem]
        t = Tl(name, ap)
        keep = []
        for (o0, o1, d) in self.dead:
            if o0 < off + n4 and off < o1:
                t.readers.extend(d.readers)
                t.readers.extend(d.writers)
            keep.append((o0, o1, d))
        self.dead = keep
        self.live.append((off, off + n4, t))
        return t

    def mark(self):
        return (self.top, len(self.live))

    def release(self, mk):
        top, nlive = mk
        for ent in self.live[nlive:]:
            self.dead.append(ent)
        del self.live[nlive:]
        self.top = top


class Prog:
    def __init__(self, cfg):
        self.cfg = cfg
        self.nc = bass.Bass("TRN2", target_bir_lowering=False)
        self.S = Sched(self.nc)
        self.es = ExitStack()
        self.dram = {}
        self.out_names = []
        self.final_ops = []

    def din(self, name, shape, dtype=F32):
        t = Tl(name, self.nc.dram_tensor(name, list(shape), dtype, kind="ExternalInput").ap())
        self.dram[name] = t
        return t

    def dscratch(self, name, shape, dtype, out=False):
        kind = "ExternalOutput" if (out or name in self.cfg.get("debug_out", ())) else "Internal"
        t = Tl(name, self.nc.dram_tensor(name, list(shape), dtype, kind=kind).ap())
        self.dram[name] = t
        if kind == "ExternalOutput":
            self.out_names.append(name)
        return t

    def setup_mem(self):
        nc = self.nc
        ncols = 51000
        big = self.es.enter_context(nc.sbuf_tensor("arena", [128, ncols], F32))
        self.A = Arena(big, ncols)
        self.psall = self.es.enter_context(nc.psum_tensor("psall", [128, 4096], F32))
        self.ps = [Tl("ps%d" % i, self.psall[:, i * 512:(i + 1) * 512]) for i in range(8)]
        self.eps_col = self.A.alloc("eps_col", 2, F32)
        self.S.op("pool", lambda e: e.memset(self.eps_col[:, :], EPS), writes=[self.eps_col])

    def mm(self, pst, out_ap, lt, lhsT, rt, rhs, start, stop):
        return self.S.op("pe", lambda e: e.matmul(out_ap, lhsT=lhsT, rhs=rhs, start=start, stop=stop),
                         reads=[lt, rt], writes=[pst])

    def tr(self, pst, out_ap, it, in_ap, ident_t, ident_ap):
        return self.S.op("pe", lambda e: e.transpose(out_ap, in_ap, ident_ap), reads=[it, ident_t], writes=[pst])

    def act(self, ot, out_ap, it, in_ap, func, scale=1.0, bias=None, extra_reads=(), accum=None, accum_t=None):
        kw = {}
        if bias is not None:
            kw["bias"] = bias
        if accum is not None:
            kw["accum_out"] = accum
        w = [ot] + ([accum_t] if accum_t is not None else [])
        return self.S.op("act", lambda e: e.activation(out=out_ap, in_=in_ap, func=func, scale=scale, **kw),
                         reads=[it] + list(extra_reads), writes=w)

    def vop(self, eng, fn, reads, writes):
        return self.S.op(eng, fn, reads=reads, writes=writes)


def make_ident(P):
    A, S, nc = P.A, P.S, P.nc
    idf = A.alloc("ident_f", 128, F32)
    idb = A.alloc("ident_b", 128, BF16)
    S.op("pool", lambda e: e.memset(idf[:], 1.0), writes=[idf])
    S.op("pool", lambda e: e.affine_select(out=idf[:], in_=idf[:], pattern=[[-1, 128]], compare_op=ALU.is_equal,
                                           fill=0.0, base=0, channel_multiplier=1), reads=[idf], writes=[idf])
    S.op("dve", lambda e: e.tensor_copy(out=idb[:], in_=idf[:]), reads=[idf], writes=[idb])
    P.identf, P.identb = idf, idb


def load_xT(P, x_t, x_ap_b, xT):
    A, S = P.A, P.S
    mk = A.mark()
    xin = [A.alloc("xin%d" % i, D, F32) for i in range(2)]
    xb = [A.alloc("xb%d" % i, D, BF16) for i in range(2)]
    for ti in range(T // 128):
        xi = xin[ti % 2]
        xbb = xb[ti % 2]
        S.dma("sp", xi[:], x_ap_b[ti * 128:(ti + 1) * 128, :], reads=[x_t], writes=[xi])
        S.op("pool", lambda e, xi=xi, xbb=xbb: e.tensor_copy(out=xbb[:], in_=xi[:]), reads=[xi], writes=[xbb])
        for g in range(4):
            pst = P.ps[(ti * 4 + g) % 4]
            pb = pst.ap.bitcast(BF16)
            for k in range(4):
                kc = g * 4 + k
                P.tr(pst, pb[:, k * 128:(k + 1) * 128], xbb, xbb[:, kc * 128:(kc + 1) * 128], P.identb, P.identb[:])
            out = xT[:, g * 4:(g + 1) * 4, ti * 128:(ti + 1) * 128]
            src = pb[:, 0:512].rearrange("p (k n) -> p k n", k=4)
            eng = "act" if g % 2 == 0 else "dve"
            if eng == "act":
                S.op("act", lambda e, out=out, src=src: e.activation(out=out, in_=src, func=AF.Copy),
                     reads=[pst], writes=[xT])
            else:
                S.op("dve", lambda e, out=out, src=src: e.tensor_copy(out=out, in_=src), reads=[pst], writes=[xT])
    A.release(mk)


def stage_proj(P, l, b, x_t, x_ap_b):
    A, S = P.A, P.S
    d = P.dram
    mk = A.mark()
    xTt = A.alloc("xT", 16 * T, BF16)
    xT = xTt.ap.rearrange("p (k t) -> p k t", k=16)
    xTt.ap = xT
    load_xT(P, x_t, x_ap_b, xTt)
    w_in = d["w_in"]
    wv = w_in.ap[l].rearrange("(k p) n -> p k n", p=128)
    WCH = 512
    wb = [A.alloc("wbuf%d" % i, 16 * WCH, BF16) for i in range(2)]
    stg = [A.alloc("stg%d" % i, T, BF16) for i in range(3)]
    stgf = [A.alloc("stgf%d" % i, 16, F32) for i in range(2)]
    fm_dsts = [
        (C_CQ, 384, d["cqT"]), (C_CKV, 256, d["ckvT"]), (C_KIDX, 64, d["kidxT"]),
        (C_U, 512, d["uT"]), (C_QC, 512, d["qcT"]), (C_KC, 512, d["kcT"]), (C_G, 6144, d["sgT"]),
    ]
    wcount = [0]
    scount = [0]

    def load_w(c0, ncol):
        t = wb[wcount[0] % 2]
        wcount[0] += 1
        v = t.ap.rearrange("p (k n) -> p k n", k=16)
        for kh in range(4):
            S.dma("pool", v[:, kh * 4:(kh + 1) * 4, 0:ncol], wv[:, kh * 4:(kh + 1) * 4, c0:c0 + ncol],
                  reads=[w_in], writes=[t])
        return t, v

    pcount = [0]
    for (c0, width, dst) in fm_dsts:
        for cc in range(0, width, WCH):
            ncol = min(WCH, width - cc)
            wt, wview = load_w(c0 + cc, ncol)
            for m0 in range(0, ncol, 128):
                mw = min(128, ncol - m0)
                st = stg[scount[0] % 3]
                scount[0] += 1
                for tc in range(4):
                    pst = P.ps[4 + pcount[0] % 4]
                    pcount[0] += 1
                    for kc in range(16):
                        P.mm(pst, pst[0:mw, :], wt, wview[:, kc, m0:m0 + mw], xTt, xT[:, kc, tc * 512:(tc + 1) * 512],
                             kc == 0, kc == 15)
                    func = AF.Sigmoid if dst is d["sgT"] else AF.Copy
                    if func == AF.Copy and tc % 2 == 1:
                        S.op("dve", lambda e, st=st, pst=pst, tc=tc, mw=mw: e.tensor_copy(
                            out=st[0:mw, tc * 512:(tc + 1) * 512], in_=pst[0:mw, :]), reads=[pst], writes=[st])
                    else:
                        S.op("act", lambda e, st=st, pst=pst, tc=tc, mw=mw, func=func: e.activation(
                            out=st[0:mw, tc * 512:(tc + 1) * 512], in_=pst[0:mw, :], func=func),
                            reads=[pst], writes=[st])
                r0 = cc + m0
                S.dma("sp", dst.ap[b, r0:r0 + mw, :], st[0:mw, :], reads=[st], writes=[dst])
    wt, wview = load_w(C_VC, 512)
    wt2, wview2 = load_w(C_WIDX, 16)
    for ti in range(T // 128):
        pst = P.ps[4 + pcount[0] % 4]
        pcount[0] += 1
        for kc in range(16):
            P.mm(pst, pst[:, :], xTt, xT[:, kc, ti * 128:(ti + 1) * 128], wt, wview[:, kc, 0:512], kc == 0, kc == 15)
        st = stg[scount[0] % 3]
        scount[0] += 1
        S.op("act", lambda e, st=st, pst=pst: e.activation(out=st[:, 0:512], in_=pst[:, :], func=AF.Copy),
             reads=[pst], writes=[st])
        S.dma("sp", d["vc"].ap[b, ti * 128:(ti + 1) * 128, :], st[:, 0:512], reads=[st], writes=[d["vc"]])
        pst = P.ps[4 + pcount[0] % 4]
        pcount[0] += 1
        for kc in range(16):
            P.mm(pst, pst[:, 0:16], xTt, xT[:, kc, ti * 128:(ti + 1) * 128], wt2, wview2[:, kc, 0:16], kc == 0, kc == 15)
        sf = stgf[ti % 2]
        S.op("dve", lambda e, sf=sf, pst=pst: e.tensor_copy(out=sf[:, :], in_=pst[:, 0:16]), reads=[pst], writes=[sf])
        S.dma("sp", d["widx"].ap[b, ti * 128:(ti + 1) * 128, :], sf[:, :], reads=[sf], writes=[d["widx"]])
    A.release(mk)


def load_param_cols(P, name, src_t, src_rows_ap, R):
    A, S = P.A, P.S
    dst = A.alloc(name, R, F32)
    mk = A.mark()
    tmp = A.alloc(name + "_ld", 128, F32)
    S.dma("sp", tmp[0:R, :], src_rows_ap, reads=[src_t], writes=[tmp])
    pst = P.ps[7]
    P.tr(pst, pst[:, 0:R], tmp, tmp[0:R, :], P.identf, P.identf[0:R, 0:R])
    S.op("dve", lambda e: e.tensor_copy(out=dst[:, :], in_=pst[:, 0:R]), reads=[pst], writes=[dst])
    A.release(mk)
    return dst


def load_bcast(P, name, src_t, vec_ap, n):
    dst = P.A.alloc(name, n, F32)
    P.S.dma("sp", dst[:, :], vec_ap.partition_broadcast(128), reads=[src_t], writes=[dst])
    return dst


def stage_ln(P, src_t, src_ap, g_t, g_ap, b_t, b_ap, dst_t, dst_ap):
    A, S = P.A, P.S
    mk = A.mark()
    gb = load_bcast(P, "ln_g", g_t, g_ap, D)
    bb = load_bcast(P, "ln_b", b_t, b_ap, D)
    vin = [A.alloc("ln_v%d" % i, D, F32) for i in range(2)]
    vo = [A.alloc("ln_o%d" % i, D, F32) for i in range(2)]
    st = [A.alloc("ln_st%d" % i, 24, F32) for i in range(2)]
    mv = [A.alloc("ln_mv%d" % i, 2, F32) for i in range(2)]
    sd = [A.alloc("ln_sd%d" % i, 2, F32) for i in range(2)]
    rs = [A.alloc("ln_rs%d" % i, 2, F32) for i in range(2)]
    nb = [A.alloc("ln_nb%d" % i, 2, F32) for i in range(2)]
    for ti in range(T // 128):
        i = ti % 2
        v, o, s_, m_, sd_, r_, n_ = vin[i], vo[i], st[i], mv[i], sd[i], rs[i], nb[i]
        S.dma("sp", v[:, :], src_ap[ti * 128:(ti + 1) * 128, :], reads=[src_t], writes=[v])
        for c in range(4):
            S.op("dve", lambda e, v=v, s_=s_, c=c: e.bn_stats(out=s_[:, c * 6:(c + 1) * 6], in_=v[:, c * 512:(c + 1) * 512]),
                 reads=[v], writes=[s_])
        S.op("dve", lambda e, s_=s_, m_=m_: e.bn_aggr(out=m_[:, 0:2], in_=s_[:, 0:24]), reads=[s_], writes=[m_])
        S.op("act", lambda e, m_=m_, sd_=sd_: e.activation(out=sd_[:, 0:1], in_=m_[:, 1:2], func=AF.Sqrt, bias=P.eps_col[:, 0:1]),
             reads=[m_, P.eps_col], writes=[sd_])
        S.op("dve", lambda e, sd_=sd_, r_=r_: e.reciprocal(out=r_[:, 0:1], in_=sd_[:, 0:1]), reads=[sd_], writes=[r_])
        S.op("dve", lambda e, m_=m_, r_=r_, n_=n_: e.tensor_scalar(out=n_[:, 0:1], in0=m_[:, 0:1], scalar1=r_[:, 0:1],
                                                                  scalar2=-1.0, op0=ALU.mult, op1=ALU.mult),
             reads=[m_, r_], writes=[n_])
        S.op("act", lambda e, v=v, o=o, r_=r_, n_=n_: e.activation(out=o[:, :], in_=v[:, :], func=AF.Identity,
                                                                 scale=r_[:, 0:1], bias=n_[:, 0:1]),
             reads=[v, r_, n_], writes=[o])
        S.op("pool", lambda e, o=o: e.tensor_tensor(out=o[:, :], in0=o[:, :], in1=gb[:, :], op=ALU.mult), reads=[o, gb], writes=[o])
        S.op("pool", lambda e, o=o: e.tensor_tensor(out=o[:, :], in0=o[:, :], in1=bb[:, :], op=ALU.add), reads=[o, bb], writes=[o])
        S.dma("sp", dst_ap[ti * 128:(ti + 1) * 128, :], o[:, :], reads=[o], writes=[dst_t])
    A.release(mk)


def stage_merge(P, l, b, x_t, x_ap_b):
    A, S = P.A, P.S
    d = P.dram
    mk = A.mark()
    wbr_t = A.alloc("wbr", 12 * D, BF16)
    wbr = wbr_t.ap.rearrange("p (k n) -> p k n", k=12)
    src = d["w_branch"].ap[l].rearrange("n (k p) c -> p (n k) c", p=128)
    for k4 in range(6):
        S.dma("pool", wbr[:, k4 * 2:(k4 + 1) * 2, :], src[:, k4 * 2:(k4 + 1) * 2, :], reads=[d["w_branch"]], writes=[wbr_t])
    wo = [A.alloc("wout%d" % i, 16 * 512, BF16) for i in range(2)]
    wov = d["w_out"].ap[l].rearrange("(k p) n -> p k n", p=128)
    oTt = A.alloc("oTc", 12 * 512, BF16)
    oTv = oTt.ap.rearrange("p (k n) -> p k n", k=12)
    mTt = A.alloc("mT", 16 * 512, BF16)
    mTv = mTt.ap.rearrange("p (k n) -> p k n", k=16)
    sg = [A.alloc("sg%d" % i, 3 * 512, BF16) for i in range(2)]
    tmp = [A.alloc("mtmp%d" % i, 512, F32) for i in range(3)]
    xin = [A.alloc("mx%d" % i, 512, F32) for i in range(2)]
    vout = [A.alloc("mv%d" % i, 512, F32) for i in range(2)]
    oTsrc = d["oT"].ap[b].rearrange("n (k p) t -> p (n k) t", p=128)
    sgsrc = d["sgT"].ap[b].rearrange("(n k p) t -> p n k t", n=3, p=128)
    pc = [0]
    wc = [0]
    xc = [0]
    for tc in range(4):
        tsl = slice(tc * 512, (tc + 1) * 512)
        S.dma("sp", oTv[:, :, :], oTsrc[:, :, tsl], reads=[d["oT"]], writes=[oTt])
        for dc in range(16):
            sgt = sg[dc % 2]
            sgv = sgt.ap.rearrange("p (n t) -> p n t", n=3)
            S.dma("act", sgv[:, :, :], sgsrc[:, :, dc, tsl], reads=[d["sgT"]], writes=[sgt])
            for n in range(3):
                pst = P.ps[pc[0] % 4]
                pc[0] += 1
                for kc in range(4):
                    P.mm(pst, pst[:, :], wbr_t, wbr[:, n * 4 + kc, dc * 128:(dc + 1) * 128], oTt, oTv[:, n * 4 + kc, :], kc == 0, kc == 3)
                tm_ = tmp[n]
                S.op("dve", lambda e, tm_=tm_, pst=pst, sgv=sgv, n=n: e.tensor_tensor(out=tm_[:, :], in0=pst[:, :], in1=sgv[:, n, :], op=ALU.mult),
                     reads=[pst, sgt], writes=[tm_])
            S.op("pool", lambda e: e.tensor_tensor(out=tmp[0][:, :], in0=tmp[0][:, :], in1=tmp[1][:, :], op=ALU.add),
                 reads=[tmp[0], tmp[1]], writes=[tmp[0]])
            S.op("pool", lambda e, dc=dc: e.tensor_tensor(out=mTv[:, dc, :], in0=tmp[0][:, :], in1=tmp[2][:, :], op=ALU.add),
                 reads=[tmp[0], tmp[2]], writes=[mTt])
        for cg in range(4):
            wt = wo[wc[0] % 2]
            wc[0] += 1
            wtv = wt.ap.rearrange("p (k n) -> p k n", k=16)
            for kh in range(4):
                S.dma("pool", wtv[:, kh * 4:(kh + 1) * 4, :], wov[:, kh * 4:(kh + 1) * 4, cg * 512:(cg + 1) * 512],
                      reads=[d["w_out"]], writes=[wt])
            for tt in range(4):
                tok = tc * 512 + tt * 128
                pst = P.ps[4 + pc[0] % 4]
                pc[0] += 1
                for dc in range(16):
                    P.mm(pst, pst[:, :], mTt, mTv[:, dc, tt * 128:(tt + 1) * 128], wt, wtv[:, dc, :], dc == 0, dc == 15)
                xi = xin[xc[0] % 2]
                vo = vout[xc[0] % 2]
                xc[0] += 1
                S.dma("sp", xi[:, :], x_ap_b[tok:tok + 128, cg * 512:(cg + 1) * 512], reads=[x_t], writes=[xi])
                S.op("dve", lambda e, xi=xi, vo=vo, pst=pst: e.scalar_tensor_tensor(out=vo[:, :], in0=xi[:, :], scalar=ALPHA, in1=pst[:, :],
                                                                                   op0=ALU.mult, op1=ALU.add),
                     reads=[xi, pst], writes=[vo])
                S.dma("sp", d["vpre"].ap[b, tok:tok + 128, cg * 512:(cg + 1) * 512], vo[:, :], reads=[vo], writes=[d["vpre"]])
    A.release(mk)


GELU_C = 0.7978845608028654


def gelu_mul(P, ta, tv, out_ap, out_t, scr):
    S = P.S
    s1, s2 = scr
    S.op("act", lambda e: e.activation(out=s1[:, :], in_=ta[:, :], func=AF.Square), reads=[ta], writes=[s1])
    S.op("pool", lambda e: e.tensor_scalar(out=s1[:, :], in0=s1[:, :], scalar1=0.044715, scalar2=1.0, op0=ALU.mult, op1=ALU.add),
         reads=[s1], writes=[s1])
    S.op("pool", lambda e: e.tensor_tensor(out=s1[:, :], in0=s1[:, :], in1=ta[:, :], op=ALU.mult), reads=[s1, ta], writes=[s1])
    S.op("act", lambda e: e.activation(out=s2[:, :], in_=s1[:, :], func=AF.Sigmoid, scale=2.0 * GELU_C), reads=[s1], writes=[s2])
    S.op("pool", lambda e: e.tensor_tensor(out=s2[:, :], in0=s2[:, :], in1=ta[:, :], op=ALU.mult), reads=[s2, ta], writes=[s2])
    S.op("dve", lambda e: e.tensor_tensor(out=out_ap, in0=s2[:, :], in1=tv[:, :], op=ALU.mult), reads=[s2, tv], writes=[out_t])


def stage_ffn(P, l, b, x1_t, x1_ap_b):
    A, S = P.A, P.S
    d = P.dram
    mk = A.mark()
    cwsrc = d["conv_w"].ap[l].rearrange("j (k p) -> (j k) p", p=128)
    cbsrc = d["conv_b"].ap[l].rearrange("(k p) -> k p", p=128)
    cw = A.alloc("cw", 264, F32)
    for r0 in range(0, 264, 128):
        R = min(128, 264 - r0)
        part = load_param_cols(P, "cwp%d" % r0, d["conv_w"], cwsrc[r0:r0 + R, :], R)
        S.op("dve", lambda e, part=part, r0=r0, R=R: e.tensor_copy(out=cw[:, r0:r0 + R], in_=part[:, 0:R]), reads=[part], writes=[cw])
    cb = load_param_cols(P, "cb", d["conv_b"], cbsrc[0:88, :], 88)
    HT = 1024
    NC = HT + 2
    x1Tt = A.alloc("x1T", 16 * NC, BF16)
    x1T = x1Tt.ap.rearrange("p (k t) -> p k t", k=16)
    actTt = A.alloc("actT", 44 * HT, BF16)
    actT = actTt.ap.rearrange("p (k t) -> p k t", k=44)
    wupv = d["w_up"].ap[l].rearrange("(k p) n -> p k n", p=128)
    wdnv = d["w_down"].ap[l].rearrange("(k p) n -> p k n", p=128)
    psA = [P.ps[0], P.ps[1], P.ps[2]]
    psV = [P.ps[3], P.ps[4], P.ps[5]]
    pall = P.psall
    for hh in range(2):
        t0 = hh * HT
        mk2 = A.mark()
        xin = [A.alloc("fx%d" % i, D, F32) for i in range(2)]
        xb = [A.alloc("fxb%d" % i, D, BF16) for i in range(2)]
        if hh == 0:
            S.op("pool", lambda e: e.memset(x1T[:, :, 0:2], 0.0), writes=[x1Tt])
        ntl = HT // 128
        tiles = list(range(ntl))
        if hh == 1:
            tiles = [-1] + tiles
        for ti in tiles:
            xi = xin[ti % 2]
            xbb = xb[ti % 2]
            if ti == -1:
                rows = slice(t0 - 128, t0)
            else:
                rows = slice(t0 + ti * 128, t0 + (ti + 1) * 128)
            S.dma("sp", xi[:, :], x1_ap_b[rows, :], reads=[x1_t], writes=[xi])
            S.op("pool", lambda e, xi=xi, xbb=xbb: e.tensor_copy(out=xbb[:, :], in_=xi[:, :]), reads=[xi], writes=[xbb])
            for g in range(4):
                pst = P.ps[6 + g % 2]
                pb = pst.ap.bitcast(BF16)
                for k in range(4):
                    kc = g * 4 + k
                    P.tr(pst, pb[:, k * 128:(k + 1) * 128], xbb, xbb[:, kc * 128:(kc + 1) * 128], P.identb, P.identb[:, :])
                srcv = pb[:, 0:512].rearrange("p (k n) -> p k n", k=4)
                if ti == -1:
                    out = x1T[:, g * 4:(g + 1) * 4, 0:2]
                    srcv = srcv[:, :, 126:128]
                else:
                    out = x1T[:, g * 4:(g + 1) * 4, 2 + ti * 128:2 + (ti + 1) * 128]
                S.op("dve", lambda e, out=out, srcv=srcv: e.tensor_copy(out=out, in_=srcv), reads=[pst], writes=[x1Tt])
        A.release(mk2)
        mk2 = A.mark()
        wu = [A.alloc("wu%d" % i, 16 * 256, BF16) for i in range(2)]
        ta = [A.alloc("ta%d" % i, HT, F32) for i in range(2)]
        tv = [A.alloc("tv%d" % i, HT, F32) for i in range(2)]
        scr = [A.alloc("gs%d" % i, HT, F32) for i in range(2)]
        for j in range(44):
            wt = wu[j % 2]
            wtv = wt.ap.rearrange("p (k n) -> p k n", k=16)
            S.dma("pool", wtv[:, :, 0:128], wupv[:, :, j * 128:(j + 1) * 128], reads=[d["w_up"]], writes=[wt])
            S.dma("pool", wtv[:, :, 128:256], wupv[:, :, DFF + j * 128:DFF + (j + 1) * 128], reads=[d["w_up"]], writes=[wt])
            for (half, pss, base) in ((0, psA, 0), (1, psV, 3)):
                for (n0, nn, bk) in ((0, 512, 0), (512, 512, 1), (1024, 2, 2)):
                    pst = pss[bk]
                    for kc in range(16):
                        P.mm(pst, pst[:, 0:nn], wt, wtv[:, kc, half * 128:(half + 1) * 128], x1Tt, x1T[:, kc, n0:n0 + nn], kc == 0, kc == 15)
            ta_, tv_ = ta[j % 2], tv[j % 2]
            for (half, pss, base, dst) in ((0, psA, 0, ta_), (1, psV, 3, tv_)):
                k = j + 44 * half
                pv = pall[:, base * 512:base * 512 + NC]
                S.op("act", lambda e, dst=dst, pv=pv, k=k: e.activation(out=dst[:, :], in_=pv[:, 2:2 + HT], func=AF.Identity,
                                                                      scale=cw[:, 2 * 88 + k:2 * 88 + k + 1], bias=cb[:, k:k + 1]),
                     reads=pss + [cw, cb], writes=[dst])
                S.op("dve", lambda e, dst=dst, pv=pv, k=k: e.scalar_tensor_tensor(out=dst[:, :], in0=pv[:, 1:1 + HT], scalar=cw[:, 88 + k:88 + k + 1],
                                                                                in1=dst[:, :], op0=ALU.mult, op1=ALU.add),
                     reads=pss + [cw, dst], writes=[dst])
                S.op("dve", lambda e, dst=dst, pv=pv, k=k: e.scalar_tensor_tensor(out=dst[:, :], in0=pv[:, 0:HT], scalar=cw[:, k:k + 1],
                                                                                in1=dst[:, :], op0=ALU.mult, op1=ALU.add),
                     reads=pss + [cw, dst], writes=[dst])
            gelu_mul(P, ta_, tv_, actT[:, j, :], actTt, scr)
        A.release(mk2)
        mk2 = A.mark()
        wd = [A.alloc("wd%d" % i, 44 * 256, BF16) for i in range(2)]
        xin = [A.alloc("dx%d" % i, 256, F32) for i in range(2)]
        vout = [A.alloc("dv%d" % i, 256, F32) for i in range(2)]
        cnt = 0
        for cg in range(8):
            wt = wd[cg % 2]
            wtv = wt.ap.rearrange("p (k n) -> p k n", k=44)
            for k4 in range(4):
                S.dma("pool", wtv[:, k4 * 11:(k4 + 1) * 11, :], wdnv[:, k4 * 11:(k4 + 1) * 11, cg * 256:(cg + 1) * 256],
                      reads=[d["w_down"]], writes=[wt])
            for tt in range(HT // 128):
                tok = t0 + tt * 128
                pst = P.ps[6 + cnt % 2]
                for kc in range(44):
                    P.mm(pst, pst[:, 0:256], actTt, actT[:, kc, tt * 128:(tt + 1) * 128], wt, wtv[:, kc, :], kc == 0, kc == 43)
                xi, vo = xin[cnt % 2], vout[cnt % 2]
                cnt += 1
                S.dma("sp", xi[:, :], x1_ap_b[tok:tok + 128, cg * 256:(cg + 1) * 256], reads=[x1_t], writes=[xi])
                S.op("dve", lambda e, xi=xi, vo=vo, pst=pst: e.scalar_tensor_tensor(out=vo[:, :], in0=xi[:, :], scalar=ALPHA, in1=pst[:, 0:256],
                                                                                   op0=ALU.mult, op1=ALU.add),
                     reads=[xi, pst], writes=[vo])
                S.dma("sp", d["vpre"].ap[b, tok:tok + 128, cg * 256:(cg + 1) * 256], vo[:, :], reads=[vo], writes=[d["vpre"]])
        A.release(mk2)
    A.release(mk)


def _bucket_thresholds():
    n = np.arange(0, 256)
    nf = np.maximum(n, 1).astype(np.float32)
    lr = np.log(nf / np.float32(16)) / np.float32(math.log(128 / 16))
    large = np.minimum(16 + (lr * np.float32(16)).astype(np.int32), 31)
    bucket = np.where(n < 16, n, large)
    taus = []
    for bkt in range(32):
        idx = np.nonzero(bucket >= bkt)[0]
        taus.append(float(idx[0]) if len(idx) else 1e9)
    return taus


def build_ebt(P):
    A, S = P.A, P.S
    d = P.dram
    rb = load_bcast(P, "rb", d["rel_bias"], d["rel_bias"].ap.rearrange("b h -> (b h)"), 512)
    P.rb = rb
    ebt_t = A.alloc("ebt", 16 * 256, BF16)
    ebt = ebt_t.ap.rearrange("p (h n) -> p h n", h=16)
    P.ebt_t, P.ebt = ebt_t, ebt
    mk = A.mark()
    dl = A.alloc("ebt_dl", 512, F32)
    S.op("dve", lambda e: e.tensor_copy(out=dl[:, 0:16], in_=rb[:, 0:16]), reads=[rb], writes=[dl])
    S.op("dve", lambda e: e.tensor_tensor(out=dl[:, 16:512], in0=rb[:, 16:512], in1=rb[:, 0:496], op=ALU.subtract), reads=[rb], writes=[dl])
    di = A.alloc("ebt_di", 256, I32)
    df = A.alloc("ebt_df", 256, F32)
    S.op("pool", lambda e: e.iota(di[:, :], pattern=[[-128, 2], [1, 128]], base=128, channel_multiplier=-1), writes=[di])
    S.op("dve", lambda e: e.tensor_copy(out=df[:, :], in_=di[:, :]), reads=[di], writes=[df])
    acc_t = A.alloc("ebt_acc", 16 * 256, F32)
    acc = acc_t.ap.rearrange("p (h n) -> p h n", h=16)
    S.op("pool", lambda e: e.memset(acc_t[:, :], 0.0), writes=[acc_t])
    m = [A.alloc("ebt_m%d" % i, 256, F32) for i in range(2)]
    taus = _bucket_thresholds()
    for bk in range(32):
        mt = m[bk % 2]
        S.op("dve", lambda e, mt=mt, bk=bk: e.tensor_single_scalar(out=mt[:, :], in_=df[:, :], scalar=(taus[bk] if bk > 0 else -1e9), op=ALU.is_ge),
             reads=[df], writes=[mt])
        for h in range(16):
            S.op("dve", lambda e, mt=mt, bk=bk, h=h: e.scalar_tensor_tensor(out=acc[:, h, :], in0=mt[:, :], scalar=dl[:, bk * 16 + h:bk * 16 + h + 1],
                                                                         in1=acc[:, h, :], op0=ALU.mult, op1=ALU.add),
                 reads=[mt, dl, acc_t], writes=[acc_t])
    S.op("act", lambda e: e.activation(out=acc_t[:, :], in_=acc_t[:, :], func=AF.Exp), reads=[acc_t], writes=[acc_t])
    cm = m[0]
    S.op("dve", lambda e: e.tensor_single_scalar(out=cm[:, :], in_=df[:, :], scalar=0.0, op=ALU.is_ge), reads=[df], writes=[cm])
    for h in range(16):
        S.op("dve", lambda e, h=h: e.tensor_tensor(out=ebt[:, h, :], in0=acc[:, h, :], in1=cm[:, :], op=ALU.mult),
             reads=[acc_t, cm], writes=[ebt_t])
    A.release(mk)


def load_fm(P, name, src_t, src_ap, nk):
    t = P.A.alloc(name, nk * T, BF16)
    v = t.ap.rearrange("p (k t) -> p k t", k=nk)
    P.S.dma("sp", v[:, :, :], src_ap.rearrange("(k p) t -> p k t", p=128), reads=[src_t], writes=[t])
    return t, v


def attn_head(P, ti, h, hb, q_t, q_ap, k_t, k_of, v_t, v_of, maskT_t, maskT, groups, ptbufs, cnt, scale, cb=None):
    S = P.S
    far = list(range(0, max(0, ti - 1)))
    near = list(range(max(0, ti - 1), ti + 1))
    chunks = [(far[i:i + 4], False) for i in range(0, len(far), 4)] + [(near, True)]
    pT_of = {}
    for (ch, is_near) in chunks:
        pst = P.ps[cnt[0] % 4]
        cnt[0] += 1
        for i, sj in enumerate(ch):
            P.mm(pst, pst[:, i * 128:(i + 1) * 128], k_t, k_of(sj), q_t, q_ap, True, True)
        pt = ptbufs[cnt[1] % len(ptbufs)]
        cnt[1] += 1
        n = len(ch) * 128
        if is_near:
            S.op("act", lambda e, pt=pt, pst=pst, n=n: e.activation(out=pt[:, 0:n], in_=pst[:, 0:n], func=AF.Exp, scale=scale),
                 reads=[pst], writes=[pt])
            ev = P.ebt[:, hb, 256 - n:256]
            S.op("dve", lambda e, pt=pt, n=n, ev=ev: e.tensor_tensor(out=pt[:, 0:n], in0=pt[:, 0:n], in1=ev, op=ALU.mult),
                 reads=[pt, P.ebt_t], writes=[pt])
        else:
            S.op("act", lambda e, pt=pt, pst=pst, n=n: e.activation(out=pt[:, 0:n], in_=pst[:, 0:n], func=AF.Exp, scale=scale,
                                                                  bias=P.rb[:, 31 * 16 + hb:31 * 16 + hb + 1]),
                 reads=[pst, P.rb], writes=[pt])
        if maskT is not None:
            mv = maskT[:, ch[0] * 128:(ch[0] + len(ch)) * 128]
            S.op("dve", lambda e, pt=pt, n=n, mv=mv: e.tensor_tensor(out=pt[:, 0:n], in0=pt[:, 0:n], in1=mv, op=ALU.mult),
                 reads=[pt, maskT_t], writes=[pt])
        for i, sj in enumerate(ch):
            pT_of[sj] = (pt, pt[:, i * 128:(i + 1) * 128])
    for gi, (sjs, pvt, col) in enumerate(groups):
        for idx, sj in enumerate(sjs):
            pt, pap = pT_of[sj]
            P.mm(pvt, pvt[:, col:col + 65], pt, pap, v_t, v_of(sj), idx == 0, idx == len(sjs) - 1)
        if cb is not None:
            cb(gi, pvt)


def o_transpose_store(P, o_tm, oT_t, oTv, ti):
    pst = P.ps[7]
    pb = pst.ap.bitcast(BF16)
    for k in range(4):
        P.tr(pst, pb[:, k * 128:(k + 1) * 128], o_tm, o_tm[:, k * 128:(k + 1) * 128], P.identb, P.identb[:, :])
    P.S.op("act", lambda e: e.activation(out=oTv[:, :, ti * 128:(ti + 1) * 128], in_=pb[:, 0:512].rearrange("p (k n) -> p k n", k=4), func=AF.Copy),
           reads=[pst], writes=[oT_t])


def load_v1(P, name, src_t, src_ap):
    A, S = P.A, P.S
    t = A.alloc(name, 16 * 8 * 65, BF16)
    v = t.ap.rearrange("p (j h c) -> p j h c", j=16, h=8)
    S.op("pool", lambda e: e.memset(t[:, :], 1.0), writes=[t])
    for j in range(16):
        S.dma("sp", v[:, j, :, 0:64],
              src_ap[j * 128:(j + 1) * 128, :].rearrange("p (h c) -> p h c", h=8), reads=[src_t], writes=[t])
    return t, v


def stage_moba(P, l, b):
    A, S = P.A, P.S
    d = P.dram
    mk = A.mark()
    q_t, qv = load_fm(P, "mq", d["qcT"], d["qcT"].ap[b], 4)
    k_t, kv = load_fm(P, "mk", d["kcT"], d["kcT"].ap[b], 4)
    v_t, vv = load_v1(P, "mv1", d["vc"], d["vc"].ap[b])
    oT_t = A.alloc("moT", 4 * T, BF16)
    oTv = oT_t.ap.rearrange("p (k t) -> p k t", k=4)
    km = A.alloc("mkm", 32, F32)
    kmb = A.alloc("mkmb", 32, BF16)
    S.op("dve", lambda e: e.tensor_reduce(out=km[:, :], in_=k_t.ap.rearrange("p (m s) -> p m s", s=256), axis=AX.X, op=ALU.add),
         reads=[k_t], writes=[km])
    S.op("dve", lambda e: e.tensor_copy(out=kmb[:, :], in_=km[:, :]), reads=[km], writes=[kmb])
    ptbufs = [A.alloc("mpt%d" % i, 512, BF16) for i in range(6)]
    o_tm = [A.alloc("motm%d" % i, 512, BF16) for i in range(2)]
    gm = A.alloc("mgm", 64, F32)
    mx = A.alloc("mmx", 64, F32)
    sel = A.alloc("msel", 64, F32)
    gw = A.alloc("mgw", 64, F32)
    tmps = [A.alloc("mtmp%d" % i, 8 * 65, F32) for i in range(2)]
    ge = A.alloc("mge", 64, F32)
    accs = [A.alloc("macc%d" % i, 66, F32) for i in range(2)]
    rcs = [A.alloc("mrc%d" % i, 2, F32) for i in range(2)]
    cnt = [0, 0]
    hc = 0
    for ti in range(P.cfg.get("moba_tiles", T // 128)):
        ob = ti // 2
        tsl = slice(ti * 128, (ti + 1) * 128)
        if ob >= 4 and P.cfg.get("moba_dbg") != 2:
            pg = P.ps[6]
            pg2 = P.ps[7]
            if P.cfg.get("moba_dbg") == 5:
                S.op("dve", lambda e, pg=pg: e.tensor_copy(out=pg[:, 0:32], in_=km[:, 0:32]), reads=[km], writes=[pg])
                S.op("dve", lambda e, pg=pg: e.tensor_copy(out=pg[:, 32:64], in_=km[:, 0:32]), reads=[km], writes=[pg])
            else:
                for h in range(8):
                    hp = (h % 2) * 64
                    pgx = pg if h % 2 == 0 else pg2
                    P.mm(pgx, pgx[:, (h // 2) * 8:(h // 2) * 8 + 8], q_t, qv[hp:hp + 64, h // 2, tsl], kmb,
                         kmb[hp:hp + 64, (h // 2) * 8:(h // 2) * 8 + 8], True, True)
            gm3 = gm.ap.rearrange("p (h n) -> p h n", h=8)
            gw3 = gw.ap.rearrange("p (h n) -> p h n", h=8)
            ge3 = ge.ap.rearrange("p (h n) -> p h n", h=8)
            sel3 = sel.ap.rearrange("p (h n) -> p h n", h=8)
            mxb = mx[:, 0:8].unsqueeze(2).to_broadcast([128, 8, 8])
            spaced(P, "dve", lambda e: e.memset(gm[:, :], -1e30), reads=(), writes=[gm])
            gm4 = gm.ap.rearrange("p (k j n) -> p k j n", k=4, j=2)
            if P.cfg.get("moba_dbg") == 5:
                spaced(P, "dve", lambda e, ob=ob, pg=pg, gm3=gm3: e.tensor_copy(out=gm3[:, :, 0:ob],
                                                                          in_=pg[:, 0:64].rearrange("p (h n) -> p h n", h=8)[:, :, 0:ob]),
                       reads=[pg, gm], writes=[gm])
            else:
                for j_, pgx in ((0, pg), (1, pg2)):
                    spaced(P, "dve", lambda e, ob=ob, pgx=pgx, gm4=gm4, j_=j_: e.tensor_copy(
                        out=gm4[:, :, j_, 0:ob], in_=pgx[:, 0:32].rearrange("p (k n) -> p k n", k=4)[:, :, 0:ob]),
                        reads=[pgx, gm], writes=[gm])
            spaced(P, "dve", lambda e: e.tensor_copy(out=gw[:, :], in_=gm[:, :]), reads=[gm], writes=[gw])
            for rnd in range(3):
                spaced(P, "dve", lambda e, gw3=gw3: e.tensor_reduce(out=mx[:, 0:8], in_=gw3, axis=AX.X, op=ALU.max), reads=[gw], writes=[mx])
                if rnd == 2:
                    break
                spaced(P, "dve", lambda e, gw3=gw3, ge3=ge3, mxb=mxb: e.tensor_tensor(out=ge3, in0=gw3, in1=mxb, op=ALU.is_ge),
                     reads=[gw, mx], writes=[ge])
                spaced(P, "dve", lambda e: e.tensor_scalar(out=ge[:, :], in0=ge[:, :], scalar1=-1e30, scalar2=0.0, op0=ALU.mult, op1=ALU.add),
                     reads=[ge], writes=[ge])
                spaced(P, "dve", lambda e: e.tensor_tensor(out=gw[:, :], in0=gw[:, :], in1=ge[:, :], op=ALU.add), reads=[gw, ge], writes=[gw])
            spaced(P, "dve", lambda e, gm3=gm3, sel3=sel3, mxb=mxb: e.tensor_tensor(out=sel3, in0=gm3, in1=mxb, op=ALU.is_ge),
                 reads=[gm, mx], writes=[sel])
        o = o_tm[ti % 2]
        for h in range(8):
            hp = (h % 2) * 64
            q_ap = qv[hp:hp + 64, h // 2, tsl]
            k_of = lambda sj, hp=hp, h=h: kv[hp:hp + 64, h // 2, sj * 128:(sj + 1) * 128]
            v_of = lambda sj, h=h: vv[:, sj, h, :]
            acc, rc = accs[hc % 2], rcs[hc % 2]
            hc += 1
            if ob < 4 or P.cfg.get("moba_dbg") in (1, 2, 4, 5):
                pvt = P.ps[4 + hc % 2]
                groups = [(list(range(ti + 1)), pvt, 0)]
                attn_head(P, ti, h, 8 + h, q_t, q_ap, k_t, k_of, v_t, v_of, None, None, groups, ptbufs, cnt, 0.125)
                src_t, src = pvt, pvt[:, 0:65]
                if ob >= 4 and P.cfg.get("moba_dbg") == 4:
                    S.op("dve", lambda e, acc=acc, pvt=pvt, h=h: e.tensor_scalar(out=acc[:, 0:65], in0=pvt[:, 0:65], scalar1=sel[:, h * 8:h * 8 + 1],
                                                                              scalar2=None, op0=ALU.mult), reads=[pvt, sel], writes=[acc])
            else:
                groups = []
                for bi in range(ob + 1):
                    sjs = [s_ for s_ in (2 * bi, 2 * bi + 1) if s_ <= ti]
                    groups.append((sjs, P.ps[4 + bi % 2], 0))

                def cb(gi, pvt, acc=acc, h=h, ob=ob):
                    if gi == 0:
                        S.op("dve", lambda e: e.tensor_scalar(out=acc[:, 0:65], in0=pvt[:, 0:65], scalar1=sel[:, h * 8:h * 8 + 1],
                                                             scalar2=None, op0=ALU.mult), reads=[pvt, sel], writes=[acc])
                    elif gi < ob:
                        S.op("dve", lambda e: e.scalar_tensor_tensor(out=acc[:, 0:65], in0=pvt[:, 0:65], scalar=sel[:, h * 8 + gi:h * 8 + gi + 1],
                                                                    in1=acc[:, 0:65], op0=ALU.mult, op1=ALU.add),
                             reads=[pvt, sel, acc], writes=[acc])
                    else:
                        S.op("dve", lambda e: e.tensor_tensor(out=acc[:, 0:65], in0=pvt[:, 0:65], in1=acc[:, 0:65], op=ALU.add),
                             reads=[pvt, acc], writes=[acc])
                attn_head(P, ti, h, 8 + h, q_t, q_ap, k_t, k_of, v_t, v_of, None, None, groups, ptbufs, cnt, 0.125, cb=cb)
                src_t, src = acc, acc[:, 0:65]
            S.op("dve", lambda e, rc=rc, src=src: e.reciprocal(out=rc[:, 0:1], in_=src[:, 64:65]), reads=[src_t], writes=[rc])
            S.op("dve", lambda e, o=o, rc=rc, src=src, h=h: e.tensor_scalar(out=o[:, h * 64:(h + 1) * 64], in0=src[:, 0:64], scalar1=rc[:, 0:1],
                                                                         scalar2=None, op0=ALU.mult), reads=[src_t, rc], writes=[o])
        o_transpose_store(P, o, oT_t, oTv, ti)
    S.dma("sp", d["oT"].ap[b, 2].rearrange("(k p) t -> p k t", p=128), oTv[:, :, :], reads=[oT_t], writes=[d["oT"]])
    A.release(mk)


def spaced(P, eng, fn, reads, writes, cyc=160):
    if not P.cfg.get("no_nop"):
        P.S.op(eng, lambda e: e.nop(cycle_cnt=cyc), reads=(), writes=())
    return P.S.op(eng, fn, reads=reads, writes=writes)


def gelu_plain(P, ta, out_ap, out_t, scr):
    S = P.S
    s1, s2 = scr
    S.op("act", lambda e: e.activation(out=s1[:, :], in_=ta[:, :], func=AF.Square), reads=[ta], writes=[s1])
    S.op("pool", lambda e: e.tensor_scalar(out=s1[:, :], in0=s1[:, :], scalar1=0.044715, scalar2=1.0, op0=ALU.mult, op1=ALU.add),
         reads=[s1], writes=[s1])
    S.op("pool", lambda e: e.tensor_tensor(out=s1[:, :], in0=s1[:, :], in1=ta[:, :], op=ALU.mult), reads=[s1, ta], writes=[s1])
    S.op("act", lambda e: e.activation(out=s2[:, :], in_=s1[:, :], func=AF.Sigmoid, scale=2.0 * GELU_C), reads=[s1], writes=[s2])
    S.op("dve", lambda e: e.tensor_tensor(out=out_ap, in0=s2[:, :], in1=ta[:, :], op=ALU.mult), reads=[s2, ta], writes=[out_t])


def s5_params(P, l, W_t, C_t, PW_t):
    A, S = P.A, P.S
    d = P.dram
    Wv = W_t.ap.rearrange("p (r k q) -> p r k q", r=2, k=16)
    Cv = C_t.ap.rearrange("p (r k c) -> p r k c", r=2, k=16)
    PWv = PW_t.ap.rearrange("p (r k j) -> p r k j", r=3, k=16)
    mk = A.mark()
    lre = load_param_cols(P, "s5lre", d["lam_re"], d["lam_re"].ap[l].rearrange("(a g) n -> a (g n)", g=2), 16)
    lim = load_param_cols(P, "s5lim", d["lam_im"], d["lam_im"].ap[l].rearrange("(a g) n -> a (g n)", g=2), 16)
    ls = load_bcast(P, "s5ls", d["log_step"], d["log_step"].ap[l], 32)
    ls3 = ls.ap.rearrange("p (k j) -> p k j", j=2)
    N = 16

    def tl(name):
        return A.alloc(name, N, F32)
    stp, lr, li, mag, c_, s_, t1, t2, inv, xr, kr, ki, p1, p2 = [tl("s5t%d" % i) for i in range(14)]
    hp_col = A.alloc("s5hp", 2, F32)
    S.op("pool", lambda e: e.memset(hp_col[:, :], math.pi / 2), writes=[hp_col])
    S.op("dve", lambda e: e.tensor_copy(out=stp[0:64, :], in_=ls3[0:64, :, 0]), reads=[ls], writes=[stp])
    S.op("dve", lambda e: e.tensor_copy(out=stp[64:128, :], in_=ls3[64:128, :, 1]), reads=[ls], writes=[stp])
    S.op("act", lambda e: e.activation(out=stp[:, :], in_=stp[:, :], func=AF.Exp), reads=[stp], writes=[stp])

    def tt(out, a, b, op, eng="dve"):
        spaced(P, eng, lambda e: e.tensor_tensor(out=out[:, :], in0=a[:, :], in1=b[:, :], op=op), reads=[a, b], writes=[out])

    tt(lr, lre, stp, ALU.mult)
    tt(li, lim, stp, ALU.mult, "pool")
    S.op("act", lambda e: e.activation(out=mag[:, :], in_=lr[:, :], func=AF.Exp), reads=[lr], writes=[mag])
    S.op("act", lambda e: e.activation(out=s_[:, :], in_=li[:, :], func=AF.Sin, scale=1.0 / 16), reads=[li], writes=[s_])
    S.op("act", lambda e: e.activation(out=c_[:, :], in_=li[:, :], func=AF.Sin, scale=1.0 / 16, bias=hp_col[:, 0:1]),
         reads=[li, hp_col], writes=[c_])
    for _ in range(4):
        tt(t1, c_, c_, ALU.mult)
        tt(t2, s_, s_, ALU.mult, "pool")
        spaced(P, "dve", lambda e: e.scalar_tensor_tensor(out=s_[:, :], in0=c_[:, :], scalar=2.0, in1=s_[:, :], op0=ALU.mult, op1=ALU.mult),
               reads=[c_, s_], writes=[s_])
        tt(c_, t1, t2, ALU.subtract, "pool")
    ar0 = PWv[:, 0, :, 0]
    ai0 = PWv[:, 1, :, 0]
    spaced(P, "dve", lambda e: e.tensor_tensor(out=ar0, in0=mag[:, :], in1=c_[:, :], op=ALU.mult), reads=[mag, c_], writes=[PW_t])
    spaced(P, "dve", lambda e: e.tensor_tensor(out=ai0, in0=mag[:, :], in1=s_[:, :], op=ALU.mult), reads=[mag, s_], writes=[PW_t])
    tt(t1, lre, lre, ALU.mult)
    tt(t2, lim, lim, ALU.mult, "pool")
    tt(t1, t1, t2, ALU.add)
    spaced(P, "dve", lambda e: e.reciprocal(out=inv[:, :], in_=t1[:, :]), reads=[t1], writes=[inv])
    spaced(P, "dve", lambda e: e.tensor_scalar(out=xr[:, :], in0=ar0, scalar1=-1.0, scalar2=None, op0=ALU.add), reads=[PW_t], writes=[xr])
    tt(p1, xr, lre, ALU.mult)
    spaced(P, "pool", lambda e: e.tensor_tensor(out=p2[:, :], in0=ai0, in1=lim[:, :], op=ALU.mult), reads=[PW_t, lim], writes=[p2])
    tt(p1, p1, p2, ALU.add)
    tt(kr, p1, inv, ALU.mult)
    spaced(P, "pool", lambda e: e.tensor_tensor(out=p1[:, :], in0=ai0, in1=lre[:, :], op=ALU.mult), reads=[PW_t, lre, kr], writes=[p1])
    tt(p2, xr, lim, ALU.mult)
    tt(p1, p1, p2, ALU.subtract)
    tt(ki, p1, inv, ALU.mult)
    for k in range(1, 11):
        a_r, a_i = PWv[:, 0, :, k - 1], PWv[:, 1, :, k - 1]
        spaced(P, "dve", lambda e, a_r=a_r: e.tensor_tensor(out=t1[:, :], in0=a_r, in1=a_r, op=ALU.mult), reads=[PW_t], writes=[t1])
        spaced(P, "pool", lambda e, a_i=a_i: e.tensor_tensor(out=t2[:, :], in0=a_i, in1=a_i, op=ALU.mult), reads=[PW_t], writes=[t2])
        spaced(P, "dve", lambda e, k=k: e.tensor_tensor(out=PWv[:, 0, :, k], in0=t1[:, :], in1=t2[:, :], op=ALU.subtract),
               reads=[t1, t2], writes=[PW_t])
        spaced(P, "dve", lambda e, k=k, a_r=a_r, a_i=a_i: e.scalar_tensor_tensor(out=PWv[:, 1, :, k], in0=a_r, scalar=2.0, in1=a_i,
                                                                               op0=ALU.mult, op1=ALU.mult), reads=[PW_t], writes=[PW_t])
    spaced(P, "dve", lambda e: e.tensor_scalar(out=PWv[:, 2, :, :], in0=PWv[:, 1, :, :], scalar1=-1.0, scalar2=None, op0=ALU.mult),
           reads=[PW_t], writes=[PW_t])
    msk = A.alloc("s5msk", 4 * 128, F32)
    S.op("pool", lambda e: e.memset(msk[:, :], 0.0), writes=[msk])
    for k in range(4):
        for j in range(2):
            c0 = k * 128 + (2 * k + j) * 16
            S.op("pool", lambda e, j=j, c0=c0: e.memset(msk[j * 64:(j + 1) * 64, c0:c0 + 16], 1.0), reads=[msk], writes=[msk])
    braw = [A.alloc("s5b%d" % i, 256, F32) for i in range(2)]
    for i, nm in enumerate(("b_re", "b_im")):
        S.dma("sp", braw[i].ap.rearrange("q (k p) -> q k p", k=16), d[nm].ap[l].rearrange("(k g) n p -> (g n) k p", g=2),
              reads=[d[nm]], writes=[braw[i]])
    krb = kr[:, :].unsqueeze(2).to_broadcast([128, 16, 16])
    kib = ki[:, :].unsqueeze(2).to_broadcast([128, 16, 16])
    bb = [A.alloc("s5bb%d" % i, 256, F32) for i in range(2)]
    u1 = A.alloc("s5u1", 256, F32)
    u2 = A.alloc("s5u2", 256, F32)
    v3 = lambda t: t.ap.rearrange("q (k p) -> q k p", k=16)
    S.op("dve", lambda e: e.tensor_tensor(out=v3(u1), in0=v3(braw[0]), in1=krb, op=ALU.mult), reads=[braw[0], kr], writes=[u1])
    S.op("dve", lambda e: e.tensor_tensor(out=v3(u2), in0=v3(braw[1]), in1=kib, op=ALU.mult), reads=[braw[1], ki], writes=[u2])
    S.op("pool", lambda e: e.tensor_tensor(out=bb[0][:, :], in0=u1[:, :], in1=u2[:, :], op=ALU.subtract), reads=[u1, u2], writes=[bb[0]])
    S.op("dve", lambda e: e.tensor_tensor(out=v3(u1), in0=v3(braw[1]), in1=krb, op=ALU.mult), reads=[braw[1], kr, bb[0]], writes=[u1])
    S.op("dve", lambda e: e.tensor_tensor(out=v3(u2), in0=v3(braw[0]), in1=kib, op=ALU.mult), reads=[braw[0], ki, bb[0]], writes=[u2])
    S.op("pool", lambda e: e.tensor_tensor(out=bb[1][:, :], in0=u1[:, :], in1=u2[:, :], op=ALU.add), reads=[u1, u2], writes=[bb[1]])
    mt = [A.alloc("s5mt%d" % i, 128, BF16) for i in range(4)]
    cnt = 0
    for r in range(2):
        for k4 in range(4):
            pst = P.ps[4 + (cnt % 2)]
            pb = pst.ap.bitcast(BF16)
            for kk in range(4):
                pr = k4 * 4 + kk
                m_ = mt[kk]
                src = v3(bb[r])[:, pr, :].unsqueeze(1).to_broadcast([128, 8, 16])
                mv = msk[:, (pr % 4) * 128:(pr % 4 + 1) * 128].rearrange("q (a p) -> q a p", a=8)
                S.op("dve", lambda e, m_=m_, src=src, mv=mv: e.tensor_tensor(out=m_.ap.rearrange("q (a p) -> q a p", a=8), in0=src, in1=mv, op=ALU.mult),
                     reads=[bb[r], msk], writes=[m_])
                P.tr(pst, pb[:, kk * 128:(kk + 1) * 128], m_, m_[:, :], P.identb, P.identb[:, :])
            S.op("act", lambda e, r=r, k4=k4, pb=pb: e.activation(out=Wv[:, r, k4 * 4:(k4 + 1) * 4, :],
                                                                  in_=pb[:, 0:512].rearrange("p (k q) -> p k q", k=4), func=AF.Copy),
                 reads=[pst], writes=[W_t])
            cnt += 1
    craw = A.alloc("s5craw", 128, F32)
    crb = A.alloc("s5crb", 128, BF16)
    t2s = A.alloc("s5t2s", 128, BF16)
    for r, nm in enumerate(("c_re", "c_im")):
        for ct in range(4):
            src = d[nm].ap[l].rearrange("g p n -> (g p) n")[ct * 128:(ct + 1) * 128, :]
            S.dma("sp", craw[:, 0:64], src, reads=[d[nm]], writes=[craw])
            S.dma("sp", craw[:, 64:128], src, reads=[d[nm]], writes=[craw])
            S.op("dve", lambda e: e.tensor_copy(out=crb[:, :], in_=craw[:, :]), reads=[craw], writes=[crb])
            pst = P.ps[6 + ct % 2]
            pb = pst.ap.bitcast(BF16)
            P.tr(pst, pb[:, 0:128], crb, crb[:, :], P.identb, P.identb[:, :])
            S.op("act", lambda e, pb=pb: e.activation(out=t2s[:, :], in_=pb[:, 0:128], func=AF.Copy), reads=[pst], writes=[t2s])
            for kk in range(4):
                pr = ct * 4 + kk
                mv = msk[:, kk * 128:(kk + 1) * 128]
                sc = 1.0 if r == 0 else -1.0
                S.op("dve", lambda e, r=r, pr=pr, mv=mv, sc=sc: e.scalar_tensor_tensor(out=Cv[:, r, pr, :], in0=t2s[:, :], scalar=sc, in1=mv,
                                                                                    op0=ALU.mult, op1=ALU.mult),
                     reads=[t2s, msk], writes=[C_t])
    A.release(mk)


def stage_s5(P, l, b, prm):
    A, S = P.A, P.S
    d = P.dram
    W_t, C_t, PW_t, dsk, bgl, wgl_t = prm
    Wv = W_t.ap.rearrange("p (r k q) -> p r k q", r=2, k=16)
    Cv = C_t.ap.rearrange("p (r k c) -> p r k c", r=2, k=16)
    PWv = PW_t.ap.rearrange("p (r k j) -> p r k j", r=3, k=16)
    wgl = wgl_t.ap.rearrange("p (k n) -> p k n", k=4)
    mk = A.mark()
    u_t, uv = load_fm(P, "s5u", d["uT"], d["uT"].ap[b], 4)
    yg_t = A.alloc("s5yg", 4 * T, BF16)
    ygv = yg_t.ap.rearrange("p (k t) -> p k t", k=4)
    oT_t = A.alloc("s5oT", 4 * T, BF16)
    oTv = oT_t.ap.rearrange("p (k t) -> p k t", k=4)
    xa = [A.alloc("s5xa%d" % i, T, F32) for i in range(2)]
    xb = [A.alloc("s5xb%d" % i, T, F32) for i in range(2)]
    sb = [A.alloc("s5sb%d" % i, T, BF16) for i in range(2)]
    yf = A.alloc("s5yf", T, F32)
    scr = [A.alloc("s5sc%d" % i, T, F32) for i in range(2)]
    sig = [A.alloc("s5sg%d" % i, 512, F32) for i in range(2)]
    ybanks = [P.ps[0], P.ps[1], P.ps[2], P.ps[3]]
    for ct in range(4):
        for pi in range(4):
            pr = ct * 4 + pi
            for tc in range(4):
                tsl = slice(tc * 512, (tc + 1) * 512)
                for r in range(2):
                    pst = P.ps[4 + (tc % 2) * 2 + r]
                    P.mm(pst, pst[:, :], W_t, Wv[:, r, pr, :], u_t, uv[:, ct, tsl], True, True)
                    if r == 0:
                        S.op("act", lambda e, pst=pst, tsl=tsl: e.activation(out=xa[0][:, tsl], in_=pst[:, :], func=AF.Copy), reads=[pst], writes=[xa[0]])
                    else:
                        S.op("act", lambda e, pst=pst, tsl=tsl: e.activation(out=xa[1][:, tsl], in_=pst[:, :], func=AF.Copy), reads=[pst], writes=[xa[1]])
            cur, nxt = xa, xb
            for k in range(11):
                m = 1 << k
                last = (k == 10)
                dst = sb if last else nxt
                ar = PWv[:, 0, pr, k:k + 1]
                ai = PWv[:, 1, pr, k:k + 1]
                nai = PWv[:, 2, pr, k:k + 1]
                for r in range(2):
                    S.op("pool", lambda e, r=r, m=m, dst=dst, cur=cur: e.tensor_copy(out=dst[r][:, 0:m], in_=cur[r][:, 0:m]), reads=[cur[r]], writes=[dst[r]])
                if last:
                    tmp = nxt
                else:
                    tmp = dst
                S.op("dve", lambda e, m=m, cur=cur, tmp=tmp, ar=ar: e.scalar_tensor_tensor(out=tmp[0][:, m:T], in0=cur[0][:, 0:T - m], scalar=ar, in1=cur[0][:, m:T],
                                                                                         op0=ALU.mult, op1=ALU.add), reads=[cur[0], PW_t], writes=[tmp[0]])
                S.op("dve", lambda e, m=m, cur=cur, tmp=tmp, dst=dst, nai=nai: e.scalar_tensor_tensor(out=dst[0][:, m:T], in0=cur[1][:, 0:T - m], scalar=nai, in1=tmp[0][:, m:T],
                                                                                                    op0=ALU.mult, op1=ALU.add), reads=[cur[1], tmp[0], PW_t], writes=[dst[0]])
                S.op("dve", lambda e, m=m, cur=cur, tmp=tmp, ar=ar: e.scalar_tensor_tensor(out=tmp[1][:, m:T], in0=cur[1][:, 0:T - m], scalar=ar, in1=cur[1][:, m:T],
                                                                                         op0=ALU.mult, op1=ALU.add), reads=[cur[1], PW_t], writes=[tmp[1]])
                S.op("dve", lambda e, m=m, cur=cur, tmp=tmp, dst=dst, ai=ai: e.scalar_tensor_tensor(out=dst[1][:, m:T], in0=cur[0][:, 0:T - m], scalar=ai, in1=tmp[1][:, m:T],
                                                                                                   op0=ALU.mult, op1=ALU.add), reads=[cur[0], tmp[1], PW_t], writes=[dst[1]])
                cur, nxt = nxt, cur
            for tc in range(4):
                tsl = slice(tc * 512, (tc + 1) * 512)
                yb = ybanks[tc]
                P.mm(yb, yb[:, :], C_t, Cv[:, 0, pr, :], sb[0], sb[0][:, tsl], pi == 0, False)
                P.mm(yb, yb[:, :], C_t, Cv[:, 1, pr, :], sb[1], sb[1][:, tsl], False, pi == 3)
        for tc in range(4):
            tsl = slice(tc * 512, (tc + 1) * 512)
            yb = ybanks[tc]
            S.op("dve", lambda e, yb=yb, tsl=tsl, ct=ct: e.scalar_tensor_tensor(out=yf[:, tsl], in0=uv[:, ct, tsl], scalar=dsk[:, ct:ct + 1], in1=yb[:, :],
                                                                              op0=ALU.mult, op1=ALU.add), reads=[u_t, dsk, yb], writes=[yf])
        gelu_plain(P, yf, ygv[:, ct, :], yg_t, scr)
    cnt = 0
    for co in range(4):
        for tc in range(4):
            tsl = slice(tc * 512, (tc + 1) * 512)
            pst = P.ps[4 + cnt % 4]
            sg = sig[cnt % 2]
            cnt += 1
            for ci in range(4):
                P.mm(pst, pst[:, :], wgl_t, wgl[:, ci, co * 128:(co + 1) * 128], yg_t, ygv[:, ci, tsl], ci == 0, ci == 3)
            S.op("act", lambda e, pst=pst, sg=sg, co=co: e.activation(out=sg[:, :], in_=pst[:, :], func=AF.Sigmoid, bias=bgl[:, co:co + 1]),
                 reads=[pst, bgl], writes=[sg])
            S.op("dve", lambda e, sg=sg, co=co, tsl=tsl: e.tensor_tensor(out=oTv[:, co, tsl], in0=ygv[:, co, tsl], in1=sg[:, :], op=ALU.mult),
                 reads=[sg, yg_t], writes=[oT_t])
    S.dma("sp", d["oT"].ap[b, 1].rearrange("(k p) t -> p k t", p=128), oTv[:, :, :], reads=[oT_t], writes=[d["oT"]])
    A.release(mk)


def s5_layer_setup(P, l):
    A, S = P.A, P.S
    d = P.dram
    W_t = A.alloc("s5W", 2 * 16 * 128, BF16)
    C_t = A.alloc("s5C", 2 * 16 * 128, BF16)
    PW_t = A.alloc("s5PW", 3 * 16 * 11, F32)
    s5_params(P, l, W_t, C_t, PW_t)
    dsk = load_param_cols(P, "s5dsk", d["d_skip"], d["d_skip"].ap[l].rearrange("(k p) -> k p", p=128), 4)
    bgl = load_param_cols(P, "s5bgl", d["b_glu"], d["b_glu"].ap[l].rearrange("(k p) -> k p", p=128), 4)
    wgl_t = A.alloc("s5wgl", 4 * 512, BF16)
    mk = A.mark()
    wgf = A.alloc("s5wgf", 4 * 512, F32)
    S.dma("sp", wgf.ap.rearrange("p (k n) -> p k n", k=4), d["w_glu"].ap[l].rearrange("(k p) n -> p k n", p=128),
          reads=[d["w_glu"]], writes=[wgf])
    S.op("dve", lambda e: e.tensor_copy(out=wgl_t[:, :], in_=wgf[:, :]), reads=[wgf], writes=[wgl_t])
    A.release(mk)
    return (W_t, C_t, PW_t, dsk, bgl, wgl_t)


def dsa_layer_setup(P, l):
    A, S = P.A, P.S
    d = P.dram
    cqg = load_param_cols(P, "dcqg", d["cq_gain"], d["cq_gain"].ap[l].rearrange("(k p) -> k p", p=128), 3)
    ckg = load_param_cols(P, "dckg", d["ckv_gain"], d["ckv_gain"].ap[l].rearrange("(k p) -> k p", p=128), 2)
    out = {}
    for nm, src, nk, ncol, g in (("wuq", "w_uq", 3, 512, cqg), ("wqi", "w_qidx", 3, 1024, cqg),
                                 ("wuk", "w_uk", 2, 512, ckg), ("wuv", "w_uv", 2, 512, ckg)):
        wt = A.alloc("d" + nm, nk * ncol, BF16)
        wv = wt.ap.rearrange("p (k n) -> p k n", k=nk)
        mk = A.mark()
        wf = A.alloc("d" + nm + "f", nk * ncol, F32)
        wfv = wf.ap.rearrange("p (k n) -> p k n", k=nk)
        S.dma("sp", wfv[:, :, :], d[src].ap[l].rearrange("(k p) n -> p k n", p=128), reads=[d[src]], writes=[wf])
        for kc in range(nk):
            S.op("dve", lambda e, wv=wv, wfv=wfv, kc=kc, g=g: e.tensor_scalar(out=wv[:, kc, :], in0=wfv[:, kc, :], scalar1=g[:, kc:kc + 1],
                                                                             scalar2=None, op0=ALU.mult), reads=[wf, g], writes=[wt])
        A.release(mk)
        out[nm] = (wt, wv)
    ones = A.alloc("dones", 128, BF16)
    S.op("dve", lambda e: e.memset(ones[:, :], 1.0), writes=[ones])
    out["ones"] = ones
    return out


def stage_dsa(P, l, b, prm):
    A, S = P.A, P.S
    d = P.dram
    ones = prm["ones"]
    mk = A.mark()
    qT_t = A.alloc("dqT", 4 * T, BF16); qT = qT_t.ap.rearrange("p (k t) -> p k t", k=4)
    kT_t = A.alloc("dkT", 4 * T, BF16); kT = kT_t.ap.rearrange("p (k t) -> p k t", k=4)
    qi_t = A.alloc("dqi", 8 * T, BF16); qi = qi_t.ap.rearrange("p (k t) -> p k t", k=8)
    kx_t = A.alloc("dkx", T, BF16)
    v_t = A.alloc("dv1", 16 * 8 * 65, BF16); vv = v_t.ap.rearrange("p (j h c) -> p j h c", j=16, h=8)
    scl_t = A.alloc("dscl", 256, F32); sgn_t = A.alloc("dsgn", 256, F32)
    rsq_c = A.alloc("drsqc", 16, F32); rskv_c = A.alloc("drskvc", 16, F32)
    S.op("dve", lambda e: e.memset(v_t[:, :], 1.0), writes=[v_t])
    for j in range(2):
        S.dma("sp", kx_t[j * 64:(j + 1) * 64, :], d["kidxT"].ap[b], reads=[d["kidxT"]], writes=[kx_t])
    mk2 = A.mark()
    cq_t, cq = load_fm(P, "dcq", d["cqT"], d["cqT"].ap[b], 3)
    ck_t, ck = load_fm(P, "dck", d["ckvT"], d["ckvT"].ap[b], 2)
    sqq_t = A.alloc("dsqq", 3 * T, BF16); sqq = sqq_t.ap.rearrange("p (k t) -> p k t", k=3)
    sqk_t = A.alloc("dsqk", 2 * T, BF16); sqk = sqk_t.ap.rearrange("p (k t) -> p k t", k=2)
    rsq_b = A.alloc("drsqb", T, F32); rskv_b = A.alloc("drskvb", T, F32)
    wi_t = A.alloc("dwi", 256, F32)
    S.dma("sp", wi_t.ap.rearrange("p (n h) -> p n h", n=16), d["widx"].ap[b].rearrange("(n p) h -> p n h", p=128), reads=[d["widx"]], writes=[wi_t])
    S.op("act", lambda e: e.activation(out=sqq_t[:, :], in_=cq_t[:, :], func=AF.Square), reads=[cq_t], writes=[sqq_t])
    S.op("act", lambda e: e.activation(out=sqk_t[:, :], in_=ck_t[:, :], func=AF.Square), reads=[ck_t], writes=[sqk_t])
    tmpc = A.alloc("dtmpc", 512, F32)
    pc = [0]

    def nextps():
        p_ = P.ps[pc[0] % 8]
        pc[0] += 1
        return p_
    for (sq_t_, sq, nk, dim, rb_) in ((sqq_t, sqq, 3, 384.0, rsq_b), (sqk_t, sqk, 2, 256.0, rskv_b)):
        for tc in range(4):
            tsl = slice(tc * 512, (tc + 1) * 512)
            pst = nextps()
            for kc in range(nk):
                P.mm(pst, pst[:, :], ones, ones[:, :], sq_t_, sq[:, kc, tsl], kc == 0, kc == nk - 1)
            S.op("act", lambda e, pst=pst, dim=dim: e.activation(out=tmpc[:, :], in_=pst[:, :], func=AF.Sqrt, scale=1.0 / dim, bias=P.eps_col[:, 0:1]),
                 reads=[pst, P.eps_col], writes=[tmpc])
            S.op("dve", lambda e, rb_=rb_, tsl=tsl: e.reciprocal(out=rb_[:, tsl], in_=tmpc[:, :]), reads=[tmpc], writes=[rb_])
    for (sq_t_, sq, nk, dim, rc_) in ((sqq_t, sqq, 3, 384.0, rsq_c), (sqk_t, sqk, 2, 256.0, rskv_c)):
        pst = nextps()
        for ti in range(16):
            for kc in range(nk):
                P.mm(pst, pst[:, ti:ti + 1], sq_t_, sq[:, kc, ti * 128:(ti + 1) * 128], ones, ones[:, 0:1], kc == 0, kc == nk - 1)
        S.op("act", lambda e, pst=pst, dim=dim: e.activation(out=tmpc[:, 0:16], in_=pst[:, 0:16], func=AF.Sqrt, scale=1.0 / dim, bias=P.eps_col[:, 0:1]),
             reads=[pst, P.eps_col], writes=[tmpc])
        S.op("dve", lambda e, rc_=rc_: e.reciprocal(out=rc_[:, 0:16], in_=tmpc[:, 0:16]), reads=[tmpc], writes=[rc_])
    wuq_t, wuq = prm["wuq"]; wqi_t, wqi = prm["wqi"]; wuk_t, wuk = prm["wuk"]; wuv_t, wuv = prm["wuv"]
    for hp in range(4):
        for tc in range(4):
            tsl = slice(tc * 512, (tc + 1) * 512)
            pst = nextps()
            for kc in range(3):
                P.mm(pst, pst[:, :], wuq_t, wuq[:, kc, hp * 128:(hp + 1) * 128], cq_t, cq[:, kc, tsl], kc == 0, kc == 2)
            S.op("dve", lambda e, pst=pst, hp=hp, tsl=tsl: e.tensor_tensor(out=qT[:, hp, tsl], in0=pst[:, :], in1=rsq_b[:, tsl], op=ALU.mult),
                 reads=[pst, rsq_b], writes=[qT_t])
            pst = nextps()
            for kc in range(2):
                P.mm(pst, pst[:, :], wuk_t, wuk[:, kc, hp * 128:(hp + 1) * 128], ck_t, ck[:, kc, tsl], kc == 0, kc == 1)
            S.op("dve", lambda e, pst=pst, hp=hp, tsl=tsl: e.tensor_tensor(out=kT[:, hp, tsl], in0=pst[:, :], in1=rskv_b[:, tsl], op=ALU.mult),
                 reads=[pst, rskv_b], writes=[kT_t])
    for j in range(8):
        for tc in range(4):
            tsl = slice(tc * 512, (tc + 1) * 512)
            pst = nextps()
            for kc in range(3):
                P.mm(pst, pst[:, :], wqi_t, wqi[:, kc, j * 128:(j + 1) * 128], cq_t, cq[:, kc, tsl], kc == 0, kc == 2)
            S.op("act", lambda e, pst=pst, j=j, tsl=tsl: e.activation(out=qi[:, j, tsl], in_=pst[:, :], func=AF.Copy), reads=[pst], writes=[qi_t])
    for sj in range(16):
        pst = nextps()
        for kc in range(2):
            P.mm(pst, pst[:, :], ck_t, ck[:, kc, sj * 128:(sj + 1) * 128], wuv_t, wuv[:, kc, :], kc == 0, kc == 1)
        S.op("act", lambda e, pst=pst, sj=sj: e.activation(out=vv[:, sj, :, 0:64], in_=pst[:, :].rearrange("p (h c) -> p h c", h=8), func=AF.Copy,
                                                          scale=rskv_c[:, sj:sj + 1]), reads=[pst, rskv_c], writes=[v_t])
    scl3 = scl_t.ap.rearrange("p (n h) -> p n h", n=16)
    S.op("act", lambda e: e.activation(out=scl_t[:, :], in_=wi_t[:, :], func=AF.Abs), reads=[wi_t], writes=[scl_t])
    S.op("dve", lambda e: e.tensor_tensor(out=scl3, in0=scl3, in1=rsq_c[:, 0:16].unsqueeze(2).to_broadcast([128, 16, 16]), op=ALU.mult),
         reads=[scl_t, rsq_c], writes=[scl_t])
    S.op("dve", lambda e: e.tensor_scalar(out=scl_t[:, :], in0=scl_t[:, :], scalar1=1.0 / 32.0, scalar2=None, op0=ALU.mult), reads=[scl_t], writes=[scl_t])
    S.op("act", lambda e: e.activation(out=sgn_t[:, :], in_=wi_t[:, :], func=AF.Sign), reads=[wi_t], writes=[sgn_t])
    A.release(mk2)
    oT_t = A.alloc("doT", 4 * T, BF16); oTv = oT_t.ap.rearrange("p (k t) -> p k t", k=4)
    scores = A.alloc("dsc", T, F32)
    work = A.alloc("dwk", T, F32)
    rbuf = [A.alloc("drb%d" % i, T, BF16) for i in range(2)]
    maskf = A.alloc("dmf", T, BF16)
    maskT_t = A.alloc("dmT", T, BF16)
    mx8 = A.alloc("dmx8", 8, F32)
    ptbufs = [A.alloc("dpt%d" % i, 512, BF16) for i in range(6)]
    o_tm = [A.alloc("dotm%d" % i, 512, BF16) for i in range(2)]
    rcs = [A.alloc("drc%d" % i, 2, F32) for i in range(2)]
    cnt = [0, 0]
    hc = 0
    ntile = P.cfg.get("dsa_tiles", T // 128)
    if ntile < T // 128:
        S.op("dve", lambda e: e.memset(oT_t[:, :], 0.0), writes=[oT_t])
    for ti in range(ntile):
        tsl = slice(ti * 128, (ti + 1) * 128)
        SL = (ti + 1) * 128
        use_mask = ti >= 2
        if use_mask:
            nch = (SL + 511) // 512
            for hi in range(16):
                hp = (hi % 2) * 64
                base = (hi % 2) * 4
                banks = [P.ps[base + c] for c in range(nch)]
                for c in range(nch):
                    w = min(512, SL - c * 512)
                    P.mm(banks[c], banks[c][:, 0:w], qi_t, qi[hp:hp + 64, hi // 2, tsl], kx_t, kx_t[hp:hp + 64, c * 512:c * 512 + w], True, True)
                rb = rbuf[hi % 2]
                S.op("act", lambda e, rb=rb, base=base, SL=SL, ti=ti, hi=hi: e.activation(out=rb[:, 0:SL], in_=P.psall[:, base * 512:base * 512 + SL], func=AF.Relu,
                                                                                       scale=scl_t[:, ti * 16 + hi:ti * 16 + hi + 1]),
                     reads=banks + [scl_t], writes=[rb])
                sg = sgn_t[:, ti * 16 + hi:ti * 16 + hi + 1]
                if hi == 0:
                    S.op("dve", lambda e, rb=rb, SL=SL, sg=sg: e.tensor_scalar(out=scores[:, 0:SL], in0=rb[:, 0:SL], scalar1=sg, scalar2=None, op0=ALU.mult),
                         reads=[rb, sgn_t], writes=[scores])
                else:
                    S.op("dve", lambda e, rb=rb, SL=SL, sg=sg: e.scalar_tensor_tensor(out=scores[:, 0:SL], in0=rb[:, 0:SL], scalar=sg, in1=scores[:, 0:SL],
                                                                                     op0=ALU.mult, op1=ALU.add), reads=[rb, sgn_t, scores], writes=[scores])
            def causal_fill(e, ti=ti):
                if getattr(P, "_fillreg", None) is None:
                    P._fillreg = e.to_reg(-1e30)
                return e.affine_select(out=scores[:, ti * 128:(ti + 1) * 128], in_=scores[:, ti * 128:(ti + 1) * 128], pattern=[[-1, 128]],
                                       compare_op=ALU.is_ge, fill=P._fillreg, base=0, channel_multiplier=1)
            S.op("pool", causal_fill, reads=[scores], writes=[scores])
            for r in range(32):
                src = scores if r == 0 else work
                S.op("dve", lambda e, src=src, SL=SL: e.max(out=mx8[:, 0:8], in_=src[:, 0:SL]), reads=[src], writes=[mx8])
                if r < 31:
                    S.op("dve", lambda e, src=src, SL=SL: e.match_replace(out=work[:, 0:SL], in_to_replace=mx8[:, 0:8], in_values=src[:, 0:SL], imm_value=-1e30),
                         reads=[src, mx8], writes=[work])
            S.op("dve", lambda e, SL=SL: e.tensor_scalar(out=maskf[:, 0:SL], in0=scores[:, 0:SL], scalar1=mx8[:, 7:8], scalar2=None, op0=ALU.is_ge),
                 reads=[scores, mx8], writes=[maskf])
            for g0 in range(0, ti + 1, 4):
                g1 = min(ti + 1, g0 + 4)
                pst = P.ps[7]
                pb = pst.ap.bitcast(BF16)
                for k, sj in enumerate(range(g0, g1)):
                    P.tr(pst, pb[:, k * 128:(k + 1) * 128], maskf, maskf[:, sj * 128:(sj + 1) * 128], P.identb, P.identb[:, :])
                n_ = (g1 - g0) * 128
                S.op("act", lambda e, pb=pb, g0=g0, n_=n_: e.activation(out=maskT_t[:, g0 * 128:g0 * 128 + n_], in_=pb[:, 0:n_], func=AF.Copy),
                     reads=[pst], writes=[maskT_t])
        o = o_tm[ti % 2]
        for h in range(8):
            hp = (h % 2) * 64
            q_ap = qT[hp:hp + 64, h // 2, tsl]
            k_of = lambda sj, hp=hp, h=h: kT[hp:hp + 64, h // 2, sj * 128:(sj + 1) * 128]
            v_of = lambda sj, h=h: vv[:, sj, h, :]
            rc = rcs[hc % 2]
            hc += 1
            pvt = P.ps[5 + hc % 2] if not use_mask else P.ps[4 + hc % 2]
            groups = [(list(range(ti + 1)), pvt, 0)]
            attn_head(P, ti, h, h, qT_t, q_ap, kT_t, k_of, v_t, v_of, maskT_t if use_mask else None, maskT_t.ap if use_mask else None,
                      groups, ptbufs, cnt, 0.125)
            S.op("dve", lambda e, rc=rc, pvt=pvt: e.reciprocal(out=rc[:, 0:1], in_=pvt[:, 64:65]), reads=[pvt], writes=[rc])
            S.op("dve", lambda e, o=o, rc=rc, pvt=pvt, h=h: e.tensor_scalar(out=o[:, h * 64:(h + 1) * 64], in0=pvt[:, 0:64], scalar1=rc[:, 0:1],
                                                                         scalar2=None, op0=ALU.mult), reads=[pvt, rc], writes=[o])
        o_transpose_store(P, o, oT_t, oTv, ti)
    S.dma("sp", d["oT"].ap[b, 0].rearrange("(k p) t -> p k t", p=128), oTv[:, :, :], reads=[oT_t], writes=[d["oT"]])
    A.release(mk)


def declare_io(P):
    P.din("x", [NB, T, D])
    P.din("rel_bias", [32, 16])
    stg = set(P.cfg.get("stages", ()))
    lite = P.cfg.get("lite", False)
    skip = set()
    if lite:
        if "proj" not in stg:
            skip.add("w_in")
        if "merge" not in stg:
            skip.update(("w_branch", "w_out"))
        if "ffn" not in stg:
            skip.update(("w_up", "w_down"))
    P.skipped = skip
    P.din("w_in", [DEPTH, D, INW] if "w_in" not in skip else [1, 128, 128])
    for n, shp in (("cq_gain", [DEPTH, 384]), ("ckv_gain", [DEPTH, 256]), ("w_uq", [DEPTH, 384, 512]),
                   ("w_uk", [DEPTH, 256, 512]), ("w_uv", [DEPTH, 256, 512]), ("w_qidx", [DEPTH, 384, 1024]),
                   ("lam_re", [DEPTH, 32, 64]), ("lam_im", [DEPTH, 32, 64]), ("log_step", [DEPTH, 32]),
                   ("b_re", [DEPTH, 32, 64, 16]), ("b_im", [DEPTH, 32, 64, 16]), ("c_re", [DEPTH, 32, 16, 64]),
                   ("c_im", [DEPTH, 32, 16, 64]), ("d_skip", [DEPTH, 512]), ("w_glu", [DEPTH, 512, 512]),
                   ("b_glu", [DEPTH, 512]), ("w_branch", [DEPTH, 3, 512, D]), ("w_out", [DEPTH, D, D]),
                   ("ln1_g", [DEPTH, D]), ("ln1_b", [DEPTH, D]), ("w_up", [DEPTH, D, 2 * DFF]),
                   ("conv_w", [DEPTH, 3, 2 * DFF]), ("conv_b", [DEPTH, 2 * DFF]), ("w_down", [DEPTH, DFF, D]),
                   ("ln2_g", [DEPTH, D]), ("ln2_b", [DEPTH, D])):
        P.din(n, shp if n not in skip else [1, 128, 128])
    for n, shp, dt in (("cqT", [NB, 384, T], BF16), ("ckvT", [NB, 256, T], BF16), ("kidxT", [NB, 64, T], BF16),
                       ("uT", [NB, 512, T], BF16), ("qcT", [NB, 512, T], BF16), ("kcT", [NB, 512, T], BF16),
                       ("sgT", [NB, 6144, T], BF16), ("vc", [NB, T, 512], BF16), ("widx", [NB, T, 16], F32),
                       ("oT", [NB, 3, 512, T], BF16), ("vpre", [NB, T, D], F32), ("xs1", [NB, T, D], F32),
                       ("xs2", [NB, T, D], F32)):
        if n in P.cfg.get("inject", ()):
            P.din(n, shp, dt)
        else:
            P.dscratch(n, shp, dt)
    P.dscratch("y", [NB, T, D], F32, out=True)


def build(cfg):
    P = Prog(cfg)
    declare_io(P)
    d = P.dram
    stages = cfg["stages"]
    with P.es:
        P.setup_mem()
        make_ident(P)
        if any(x in stages for x in ("moba", "dsa")):
            build_ebt(P)
        if cfg.get("zero_oT"):
            zero_fill_oT(P)
        layers = list(cfg.get("layers", range(DEPTH)))
        for l in layers:
            mk_layer = P.A.mark()
            s5prm = s5_layer_setup(P, l) if "s5" in stages else None
            dsaprm = dsa_layer_setup(P, l) if "dsa" in stages else None
            x_t = d["x"] if l == 0 else d["xs2"]
            for b in cfg.get("batches", range(NB)):
                x_ap = x_t.ap[b]
                if "proj" in stages:
                    stage_proj(P, l, b, x_t, x_ap)
                if "dsa" in stages:
                    stage_dsa(P, l, b, dsaprm)
                if "s5" in stages:
                    stage_s5(P, l, b, s5prm)
                if "moba" in stages:
                    stage_moba(P, l, b)
            P.A.release(mk_layer)
            for b in cfg.get("batches", range(NB)):
                x_ap = x_t.ap[b]
                if "merge" in stages:
                    stage_merge(P, l, b, x_t, x_ap)
                if "ln1" in stages:
                    stage_ln(P, d["vpre"], d["vpre"].ap[b], d["ln1_g"], d["ln1_g"].ap[l], d["ln1_b"], d["ln1_b"].ap[l],
                             d["xs1"], d["xs1"].ap[b])
                if "ffn" in stages:
                    stage_ffn(P, l, b, d["xs1"], d["xs1"].ap[b])
                if "ln2" in stages:
                    dst = d["y"] if l == DEPTH - 1 else d["xs2"]
                    stage_ln(P, d["vpre"], d["vpre"].ap[b], d["ln2_g"], d["ln2_g"].ap[l], d["ln2_b"], d["ln2_b"].ap[l],
                             dst, dst.ap[b])
        for n in P.out_names:
            P.final_ops.extend(P.dram[n].writers)
        P.S.emit(final_ops=P.final_ops)
    return P


FULL_STAGES = ("proj", "dsa", "s5", "moba", "merge", "ln1", "ffn", "ln2")


def zero_fill_oT(P):
    A, S = P.A, P.S
    d = P.dram
    mk = A.mark()
    z = A.alloc("zfill", T, BF16)
    S.op("pool", lambda e: e.memset(z[:, :], 0.0), writes=[z])
    for b in range(NB):
        for n in range(3):
            for k in range(4):
                S.dma("sp", d["oT"].ap[b, n, k * 128:(k + 1) * 128, :], z[:, :], reads=[z], writes=[d["oT"]])
    A.release(mk)


def kernel(**inputs):
    cfg = dict(stages=list(FULL_STAGES))
    P = build(cfg)
    x = np.ascontiguousarray(inputs["x"], dtype=np.float32)
    shared = {}
    for n, a in inputs.items():
        if n == "x":
            continue
        a = np.asarray(a, dtype=np.float32)
        if n in ("w_uq", "w_uk", "w_uv", "w_qidx"):
            a = a.reshape(a.shape[0], a.shape[1], -1)
        shared[n] = np.ascontiguousarray(a)
    in_maps = []
    for c in range(NCORES):
        m = dict(shared)
        m["x"] = np.ascontiguousarray(x[c * NB:(c + 1) * NB])
        in_maps.append(m)
    res = run_bass_kernel_spmd(P.nc, in_maps, core_ids=list(range(NCORES)))
    return np.concatenate([np.asarray(r["y"], dtype=np.float32) for r in res.results], axis=0)
```
